# Optimizing a Trainium2 kernel written in Bass

```python
import math
import jax, jax.numpy as jnp
from jax import lax
import numpy as np

D_MODEL = 1024
BATCH = 16
SEQ = 2048
DEPTH = 2

N_EVEN = (DEPTH + 1) // 2
N_ODD = DEPTH // 2

MLA_HEADS = 8
Q_LORA = 256
KV_LORA = 128
QK_NOPE = 64
QK_ROPE = 32
V_HEAD = 64
MLA_WIDTH = MLA_HEADS * V_HEAD
ROPE_BASE = 10000.0
Q_BLOCK = 128

LRU_HEADS = 8
LRU_WIDTH = 512
LRU_BLOCK = LRU_WIDTH // LRU_HEADS
LRU_CONV = 4
LRU_C = 8.0

AB_IN = Q_LORA + KV_LORA + QK_ROPE + 2 * LRU_WIDTH
AB_MIX = MLA_WIDTH + LRU_WIDTH

CHUNK = 128
SGU_GROUPS = 8
SGU_WIDTH = D_MODEL
SGU_GROUP_DIM = SGU_WIDTH // SGU_GROUPS

D_FF = 2816
FFN_CONV = 3

NORM_EPS = 1e-6

kernel_name = "hybrid_mla_rglru_chunksgu_convffn"


def rms_norm(x, g):
    xf = x.astype(jnp.float32)
    y = xf * lax.rsqrt(jnp.mean(xf * xf, axis=-1, keepdims=True) + NORM_EPS)
    return (y * g.astype(jnp.float32)).astype(x.dtype)


def layer_norm(x, g, b):
    xf = x.astype(jnp.float32)
    mu = jnp.mean(xf, axis=-1, keepdims=True)
    xc = xf - mu
    y = xc * lax.rsqrt(jnp.mean(xc * xc, axis=-1, keepdims=True) + NORM_EPS)
    return (y * g.astype(jnp.float32) + b.astype(jnp.float32)).astype(x.dtype)


def causal_dwconv(x, w, b):
    K = w.shape[0]
    S = x.shape[1]
    xp = jnp.pad(x, ((0, 0), (K - 1, 0), (0, 0)))
    y = xp[:, 0:S] * w[0]
    for k in range(1, K):
        y = y + xp[:, k:k + S] * w[k]
    return y + b


def rope(x, positions):
    half = x.shape[-1] // 2
    inv_freq = jnp.exp(-math.log(ROPE_BASE) * jnp.arange(half, dtype=jnp.float32) / half)
    ang = positions.astype(jnp.float32)[..., None] * inv_freq
    cos = jnp.cos(ang)[:, :, None, :]
    sin = jnp.sin(ang)[:, :, None, :]
    xf = x.astype(jnp.float32)
    x1, x2 = xf[..., :half], xf[..., half:]
    return jnp.concatenate([x1 * cos - x2 * sin, x2 * cos + x1 * sin], axis=-1).astype(x.dtype)


def causal_block_attention(q, k, v):
    B, S, H, Dk = q.shape
    Dv = v.shape[-1]
    nb = S // Q_BLOCK
    scale = Dk ** -0.5
    qb = q.reshape(B, nb, Q_BLOCK, H, Dk).transpose(1, 0, 2, 3, 4)
    kpos = jnp.arange(S)

    def one_block(args):
        q_blk, blk = args
        s = jnp.einsum('bqhd,bkhd->bhqk', q_blk, k,
                       preferred_element_type=jnp.float32) * scale
        qpos = blk * Q_BLOCK + jnp.arange(Q_BLOCK)
        s = jnp.where(kpos[None, :] <= qpos[:, None], s, -jnp.inf)
        p = jax.nn.softmax(s, axis=-1)
        return jnp.einsum('bhqk,bkhd->bqhd', p.astype(v.dtype), v)

    o = lax.map(one_block, (qb, jnp.arange(nb)))
    return o.transpose(1, 0, 2, 3, 4).reshape(B, S, H * Dv)


def rg_lru(x, w_a, b_a, w_x, b_x, lam):
    B, S, C = x.shape
    xg = x.reshape(B, S, LRU_HEADS, LRU_BLOCK)
    r = jax.nn.sigmoid(jnp.einsum('bsgi,gij->bsgj', xg, w_a).reshape(B, S, C) + b_a).astype(jnp.float32)
    i = jax.nn.sigmoid(jnp.einsum('bsgi,gij->bsgj', xg, w_x).reshape(B, S, C) + b_x).astype(jnp.float32)
    log_a = -LRU_C * r * jax.nn.softplus(-lam.astype(jnp.float32))
    a = jnp.exp(log_a)
    bx = jnp.sqrt(-jnp.expm1(2.0 * log_a)) * (i * x.astype(jnp.float32))

    def combine(left, right):
        a_l, b_l = left
        a_r, b_r = right
        return a_l * a_r, a_r * b_l + b_r

    _, h = lax.associative_scan(combine, (a, bx), axis=1)
    return h.astype(x.dtype)


def mla_lru_mixer(h, positions, w_in, q_norm, w_q_b, kv_norm, w_kv_b, conv_w, conv_b,
                  w_rg_a, b_rg_a, w_rg_x, b_rg_x, lam, w_out):
    B, S, _ = h.shape
    z = h @ w_in
    o1 = Q_LORA
    o2 = o1 + KV_LORA
    o3 = o2 + QK_ROPE
    o4 = o3 + LRU_WIDTH
    c_q, c_kv, k_pe, x_lru, gate_lru = jnp.split(z, [o1, o2, o3, o4], axis=-1)

    q = (rms_norm(c_q, q_norm) @ w_q_b).reshape(B, S, MLA_HEADS, QK_NOPE + QK_ROPE)
    q = jnp.concatenate([q[..., :QK_NOPE], rope(q[..., QK_NOPE:], positions)], axis=-1)
    kv = (rms_norm(c_kv, kv_norm) @ w_kv_b).reshape(B, S, MLA_HEADS, QK_NOPE + V_HEAD)
    k_pe = jnp.broadcast_to(rope(k_pe[:, :, None, :], positions), (B, S, MLA_HEADS, QK_ROPE))
    k = jnp.concatenate([kv[..., :QK_NOPE], k_pe], axis=-1)
    v = kv[..., QK_NOPE:]
    y_mla = causal_block_attention(q, k, v)

    xc = causal_dwconv(x_lru, conv_w, conv_b)
    y_lru = rg_lru(xc, w_rg_a, b_rg_a, w_rg_x, b_rg_x, lam) * jax.nn.gelu(gate_lru)

    return jnp.concatenate([y_mla, y_lru], axis=-1) @ w_out


def chunk_sgu_mixer(h, w_in, ln_g, ln_b, w_s, b_s, w_out):
    B, S, _ = h.shape
    z = jax.nn.gelu(h @ w_in)
    u, v = jnp.split(z, 2, axis=-1)
    v = layer_norm(v, ln_g, ln_b).reshape(B, S // CHUNK, CHUNK, SGU_GROUPS, SGU_GROUP_DIM)
    causal = jnp.tril(jnp.ones((CHUNK, CHUNK), dtype=w_s.dtype))
    s = jnp.einsum('gts,bnsgc->bntgc', w_s * causal, v) + b_s.T[:, :, None]
    return (u * s.reshape(B, S, SGU_WIDTH)) @ w_out


def conv_ffn(h, w_gate, w_up, conv_w, conv_b, w_down):
    g = causal_dwconv(h @ w_gate, conv_w, conv_b)
    return (jax.nn.gelu(g) * (h @ w_up)) @ w_down


def setup_inputs(seed: int = 0) -> dict:
    key = jax.random.key(seed)
    ks = jax.random.split(key, 32)
    f32 = jnp.float32

    def nrm(k, shape, fan_in):
        return jax.random.normal(k, shape, f32) * (fan_in ** -0.5)

    def gain(k, shape, s=0.02):
        return 1.0 + s * jax.random.normal(k, shape, f32)

    def bias(k, shape):
        return 0.02 * jax.random.normal(k, shape, f32)

    x = jax.random.normal(ks[0], (BATCH, SEQ, D_MODEL), f32)
    positions = jnp.broadcast_to(jnp.arange(SEQ, dtype=jnp.int32), (BATCH, SEQ))

    a0 = jax.random.uniform(ks[13], (N_EVEN, LRU_WIDTH), f32, minval=0.9, maxval=0.999)
    lam = jnp.log(a0) - jnp.log1p(-a0)

    return {
        "x": x,
        "positions": positions,
        "ab_norm": gain(ks[1], (N_EVEN, D_MODEL)),
        "ab_w_in": nrm(ks[2], (N_EVEN, D_MODEL, AB_IN), D_MODEL),
        "ab_q_norm": gain(ks[3], (N_EVEN, Q_LORA)),
        "ab_w_q_b": nrm(ks[4], (N_EVEN, Q_LORA, MLA_HEADS * (QK_NOPE + QK_ROPE)), Q_LORA),
        "ab_kv_norm": gain(ks[5], (N_EVEN, KV_LORA)),
        "ab_w_kv_b": nrm(ks[6], (N_EVEN, KV_LORA, MLA_HEADS * (QK_NOPE + V_HEAD)), KV_LORA),
        "ab_conv_w": nrm(ks[7], (N_EVEN, LRU_CONV, LRU_WIDTH), LRU_CONV),
        "ab_conv_b": bias(ks[8], (N_EVEN, LRU_WIDTH)),
        "ab_w_rg_a": nrm(ks[9], (N_EVEN, LRU_HEADS, LRU_BLOCK, LRU_BLOCK), LRU_BLOCK),
        "ab_b_rg_a": bias(ks[10], (N_EVEN, LRU_WIDTH)),
        "ab_w_rg_x": nrm(ks[11], (N_EVEN, LRU_HEADS, LRU_BLOCK, LRU_BLOCK), LRU_BLOCK),
        "ab_b_rg_x": bias(ks[12], (N_EVEN, LRU_WIDTH)),
        "ab_lambda": lam,
        "ab_w_out": nrm(ks[14], (N_EVEN, AB_MIX, D_MODEL), AB_MIX),
        "c_norm": gain(ks[15], (N_ODD, D_MODEL)),
        "c_w_in": nrm(ks[16], (N_ODD, D_MODEL, 2 * SGU_WIDTH), D_MODEL),
        "c_ln_g": gain(ks[17], (N_ODD, SGU_WIDTH)),
        "c_ln_b": bias(ks[18], (N_ODD, SGU_WIDTH)),
        "c_w_s": nrm(ks[19], (N_ODD, SGU_GROUPS, CHUNK, CHUNK), CHUNK),
        "c_b_s": gain(ks[20], (N_ODD, SGU_GROUPS, CHUNK), 0.1),
        "c_w_out": nrm(ks[21], (N_ODD, SGU_WIDTH, D_MODEL), SGU_WIDTH),
        "ffn_norm": gain(ks[22], (DEPTH, D_MODEL)),
        "ffn_w_gate": nrm(ks[23], (DEPTH, D_MODEL, D_FF), D_MODEL),
        "ffn_w_up": nrm(ks[24], (DEPTH, D_MODEL, D_FF), D_MODEL),
        "ffn_conv_w": nrm(ks[25], (DEPTH, FFN_CONV, D_FF), FFN_CONV),
        "ffn_conv_b": bias(ks[26], (DEPTH, D_FF)),
        "ffn_w_down": nrm(ks[27], (DEPTH, D_FF, D_MODEL), D_FF),
        "final_norm": gain(ks[28], (D_MODEL,)),
    }


def reference(x, positions, ab_norm, ab_w_in, ab_q_norm, ab_w_q_b, ab_kv_norm, ab_w_kv_b,
              ab_conv_w, ab_conv_b, ab_w_rg_a, ab_b_rg_a, ab_w_rg_x, ab_b_rg_x, ab_lambda,
              ab_w_out, c_norm, c_w_in, c_ln_g, c_ln_b, c_w_s, c_b_s, c_w_out,
              ffn_norm, ffn_w_gate, ffn_w_up, ffn_conv_w, ffn_conv_b, ffn_w_down, final_norm):
    h = x
    for layer in range(DEPTH):
        if layer % 2 == 0:
            i = layer // 2
            h = h + mla_lru_mixer(rms_norm(h, ab_norm[i]), positions, ab_w_in[i],
                                  ab_q_norm[i], ab_w_q_b[i], ab_kv_norm[i], ab_w_kv_b[i],
                                  ab_conv_w[i], ab_conv_b[i], ab_w_rg_a[i], ab_b_rg_a[i],
                                  ab_w_rg_x[i], ab_b_rg_x[i], ab_lambda[i], ab_w_out[i])
        else:
            i = layer // 2
            h = h + chunk_sgu_mixer(rms_norm(h, c_norm[i]), c_w_in[i], c_ln_g[i], c_ln_b[i],
                                    c_w_s[i], c_b_s[i], c_w_out[i])
        h = h + conv_ffn(rms_norm(h, ffn_norm[layer]), ffn_w_gate[layer], ffn_w_up[layer],
                         ffn_conv_w[layer], ffn_conv_b[layer], ffn_w_down[layer])
    return rms_norm(h, final_norm)
```

```python
import concourse.bass as bass
import concourse.mybir as mybir

F32 = mybir.dt.float32
BF16 = mybir.dt.bfloat16
I32 = mybir.dt.int32
AF = mybir.ActivationFunctionType
ALU = mybir.AluOpType

ENGS = ("pe", "act", "dve", "pool", "sp")


def _box(ap):
    t = ap.tensor
    name = t.name
    dims = ap.ap
    off = int(ap.offset)
    sp = str(ap.space)
    if "DRAM" in sp.upper() or "HBM" in sp.upper() or "dram" in sp:
        if len(dims) == 2 and dims[1][0] == 1 and dims[0][0] >= dims[1][1]:
            rs = dims[0][0]
            p0 = off // rs
            lo = off - p0 * rs
            return (name, p0, p0 + dims[0][1], lo, lo + dims[1][1])
        ext = 1
        for st, n in dims:
            ext += (n - 1) * abs(st)
        return (name, 0, 1 << 30, off, off + ext)
    pstep, pn = dims[0]
    if pstep == 0:
        pstep = 1 << 40
    p0 = off // pstep if pn > 1 or pstep < (1 << 40) else 0
    lo = off - p0 * pstep
    ext = 1
    for st, n in dims[1:]:
        ext += (n - 1) * abs(st)
    if "PSUM" in sp.upper():
        return (name, 0, 128, 0, 1 << 30)
    return (name, p0, p0 + pn, lo, lo + ext)


def _overlap(a, b):
    return a[1] < b[2] and b[1] < a[2] and a[3] < b[4] and b[3] < a[4]


def _covers(a, b):
    return a[1] <= b[1] and a[2] >= b[2] and a[3] <= b[3] and a[4] >= b[4]


class Op:
    __slots__ = ("id", "eng", "fn", "deps", "is_dma", "sig", "cnt", "sem", "val", "qidx", "tag")


class Prog:
    def __init__(self, nc, n_dma_sems=None):
        self.nc = nc
        self.ops = []
        self.eng_ops = {e: [] for e in ENGS}
        self.acc = {}
        self.n_dma_sems = n_dma_sems or {"sp": 24, "pool": 8, "act": 4}
        self.dma_count = {e: 0 for e in ENGS}

    def _record(self, eng, fn, reads, writes, is_dma=False, tag=None):
        op = Op()
        op.id = len(self.ops)
        op.eng = eng
        op.fn = fn
        op.deps = {}
        op.is_dma = is_dma
        op.sig = False
        op.cnt = None
        op.sem = None
        op.val = None
        op.qidx = None
        op.tag = tag
        if is_dma:
            op.qidx = self.dma_count[eng]
            self.dma_count[eng] += 1
        for ap in reads:
            bx = _box(ap)
            lst = self.acc.setdefault(bx[0], [])
            for rec in lst:
                if rec[1] and _overlap(rec[0], bx):
                    self._dep(op, rec[2], "RAW")
            if not is_dma:
                for rec in lst:
                    if (not rec[1]) and rec[0] == bx and rec[2].eng == eng and not rec[2].is_dma:
                        rec[2] = op
                        break
                else:
                    lst.append([bx, False, op])
            else:
                lst.append([bx, False, op])
        for ap in writes:
            bx = _box(ap)
            lst = self.acc.setdefault(bx[0], [])
            keep = []
            for rec in lst:
                if _overlap(rec[0], bx):
                    if rec[2] is not op:
                        self._dep(op, rec[2], "WAW" if rec[1] else "WAR")
                    if _covers(bx, rec[0]):
                        continue
                keep.append(rec)
            keep.append([bx, True, op])
            self.acc[bx[0]] = keep
        self.ops.append(op)
        self.eng_ops[eng].append(op)
        return op

    def _dep(self, op, prod, kind):
        if prod is op:
            return
        if (not prod.is_dma) and (not op.is_dma) and prod.eng == op.eng:
            if op.eng == "pe":
                return
        old = op.deps.get(prod.id)
        if old is None or kind == "RAW":
            op.deps[prod.id] = kind

    def pe(self, fn, reads, writes, tag=None):
        return self._record("pe", fn, reads, writes, tag=tag)

    def act(self, fn, reads, writes, tag=None):
        return self._record("act", fn, reads, writes, tag=tag)

    def dve(self, fn, reads, writes, tag=None):
        return self._record("dve", fn, reads, writes, tag=tag)

    def pool(self, fn, reads, writes, tag=None):
        return self._record("pool", fn, reads, writes, tag=tag)

    def dma(self, q, out, in_, **kw):
        return self._record(q, lambda e: e.dma_start(out=out, in_=in_, **kw), [in_], [out], is_dma=True)

    def emit(self, final_wait_all=True):
        nc = self.nc
        ops = self.ops
        for op in ops:
            for pid in op.deps:
                p = ops[pid]
                if not p.is_dma:
                    p.sig = True
        for e in ENGS:
            c = 0
            for op in self.eng_ops[e]:
                if not op.is_dma and op.sig:
                    c += 1
                    op.cnt = c
        from contextlib import ExitStack

        with ExitStack() as es:
            esem = {e: es.enter_context(nc.semaphore("sem_" + e)) for e in ENGS}
            dsem = {}
            for q, n in self.n_dma_sems.items():
                if self.dma_count[q] > 0:
                    dsem[q] = [es.enter_context(nc.semaphore("dsem_%s_%d" % (q, i))) for i in range(n)]
            for op in ops:
                if op.is_dma:
                    n = len(dsem[op.eng])
                    op.sem = dsem[op.eng][op.qidx % n]
                    op.val = 16 * (op.qidx // n + 1)
            block = es.enter_context(nc.Block())
            engobj = {"pe": "tensor", "act": "scalar", "dve": "vector", "pool": "gpsimd", "sp": "sync"}

            def make(ename):
                def body(eng):
                    waited = {}

                    def wait(sem, key, val):
                        if waited.get(key, 0) >= val:
                            return
                        eng.wait_ge(sem, val)
                        waited[key] = val

                    for op in self.eng_ops[ename]:
                        for pid, kind in op.deps.items():
                            p = ops[pid]
                            if p.is_dma:
                                wait(p.sem, ("d", p.sem.num), p.val)
                            else:
                                wait(esem[p.eng], ("e", p.eng), p.cnt)
                        if op.is_dma:
                            n = len(dsem[ename])
                            if op.qidx >= n:
                                wait(op.sem, ("d", op.sem.num), op.val - 16)
                            ins = op.fn(eng)
                            ins.then_inc(op.sem, 16)
                        else:
                            ins = op.fn(eng)
                            if op.sig:
                                ins.then_inc(esem[ename], 1)
                    if final_wait_all:
                        if ename in dsem:
                            n = len(dsem[ename])
                            tot = self.dma_count[ename]
                            for i in range(min(n, tot)):
                                cnt_i = len(range(i, tot, n))
                                wait(dsem[ename][i], ("d", dsem[ename][i].num), 16 * cnt_i)

                return body

            for e in ENGS:
                if self.eng_ops[e]:
                    getattr(block, engobj[e])(make(e))


import math
import numpy as np
from contextlib import ExitStack
from concourse.bass_utils import run_bass_kernel_spmd

TB = 512
D = 1024
DFF = 2816
NFC = 22
SCALE = 96 ** -0.5
EPS = 1e-6
MAGIC = 12582912.0
GELU = AF.Gelu_apprx_tanh
PIECE_MAX = 4096
RING_COLS = 4 * 4096
TW = 520
NTMP = 7


def weight_plan():
    u = []
    for n in ("cq0", "cq1", "ckv"):
        u.append(("in_" + n, 8 * 128))
    u.append(("in_kpe", 8 * 128))
    u.append(("in_kpr", 8 * 128))
    for c in range(4):
        u.append(("in_xl%d" % c, 8 * 128))
        u.append(("in_gt%d" % c, 8 * 128))
    for h in range(8):
        u.append(("wq%d" % h, 2 * 128))
        u.append(("wqr%d" % h, 2 * 128))
    for h in range(8):
        u.append(("wk%d" % h, 128))
    u.append(("wv", 512))
    for c in range(4):
        u.append(("wa%d" % c, 128))
        u.append(("wx%d" % c, 128))
    for c in range(4):
        u.append(("wom%d" % c, 1024))
    for c in range(4):
        u.append(("wol%d" % c, 1024))
    for l in range(2):
        if l == 1:
            for uc in range(8):
                u.append(("cu%d" % uc, 8 * 128))
            for hf in range(2):
                u.append(("cv%d" % hf, 8 * 512))
            for oc in range(8):
                u.append(("co%d" % oc, 8 * 128))
        for fc in range(NFC):
            u.append(("fg%d_%d" % (l, fc), 8 * 128))
            u.append(("fu%d_%d" % (l, fc), 8 * 128))
        for oc in range(8):
            u.append(("fd%d_%d" % (l, oc), NFC * 128))
    off = {}
    pieces = []
    cur = 0
    pstart, pcols, pun = 0, 0, []
    for name, c in u:
        if pcols + c > PIECE_MAX and pun:
            pieces.append((pstart, pcols, pun))
            pstart, pcols, pun = cur, 0, []
        off[name] = (cur, c, len(pieces))
        pun.append(name)
        pcols += c
        cur += c
    pieces.append((pstart, pcols, pun))
    return u, off, pieces, cur


def param_plan():
    names = [("ab_norm", 8), ("q_norm", 2), ("kv_norm", 1), ("lconv_w", 16), ("lconv_b", 4),
             ("b_a", 4), ("b_x", 4), ("lam", 4), ("c_norm", 8), ("ffn_norm0", 8), ("ffn_norm1", 8),
             ("fconv_w0", 66), ("fconv_w1", 66), ("fconv_b0", 22), ("fconv_b1", 22), ("final_norm", 8),
             ("invf", 1), ("sgn2pi", 1), ("ln_g", 8), ("ln_b", 8)]
    off = {}
    c = 0
    for n, k in names:
        off[n] = c
        c += k
    return off, c


def fm(v):
    v = np.asarray(v, np.float32)
    return np.ascontiguousarray(v.reshape(-1, 128).T)


def pack_params(inp):
    off, n = param_plan()
    p = np.zeros((128, n), np.float32)

    def put(name, arr):
        arr = np.asarray(arr, np.float32)
        p[:, off[name]:off[name] + arr.shape[1]] = arr

    put("ab_norm", fm(inp["ab_norm"][0]))
    put("q_norm", fm(inp["ab_q_norm"][0]))
    put("kv_norm", fm(inp["ab_kv_norm"][0]))
    cw = np.asarray(inp["ab_conv_w"][0], np.float32)
    lw = np.zeros((128, 16), np.float32)
    for c in range(4):
        for k in range(4):
            lw[:, c * 4 + k] = cw[k, c * 128:(c + 1) * 128]
    put("lconv_w", lw)
    put("lconv_b", fm(inp["ab_conv_b"][0]))
    put("b_a", fm(inp["ab_b_rg_a"][0]))
    put("b_x", fm(inp["ab_b_rg_x"][0]))
    put("lam", fm(inp["ab_lambda"][0]))
    put("c_norm", fm(inp["c_norm"][0]))
    put("ffn_norm0", fm(inp["ffn_norm"][0]))
    put("ffn_norm1", fm(inp["ffn_norm"][1]))
    for l in range(2):
        fw_ = np.asarray(inp["ffn_conv_w"][l], np.float32)
        a = np.zeros((128, 66), np.float32)
        for fc in range(NFC):
            for k in range(3):
                a[:, fc * 3 + k] = fw_[k, fc * 128:(fc + 1) * 128]
        put("fconv_w%d" % l, a)
        put("fconv_b%d" % l, fm(inp["ffn_conv_b"][l]))
    put("final_norm", fm(inp["final_norm"]))
    invf = np.exp(-math.log(10000.0) * np.arange(16, dtype=np.float32) / 16).astype(np.float32)
    iv = np.zeros((128, 1), np.float32)
    sg = np.zeros((128, 1), np.float32)
    twopi = np.float32(6.2831845)
    for r in range(32):
        iv[64 + r, 0] = invf[r % 16] / np.float32(2 * math.pi)
        sg[64 + r, 0] = -twopi if r < 16 else twopi
    put("invf", iv)
    put("sgn2pi", sg)
    put("ln_g", fm(inp["c_ln_g"][0]))
    put("ln_b", fm(inp["c_ln_b"][0]))
    return p


def pack_weights(inp):
    u, off, pieces, tot = weight_plan()
    W = np.zeros((128, tot), np.float32)

    def put(name, arr):
        o, c, _ = off[name]
        assert arr.shape == (128, c), (name, arr.shape, c)
        W[:, o:o + c] = arr

    def kmaj(w, cols):
        K = w.shape[0]
        sub = w[:, cols].reshape(K // 128, 128, len(cols))
        return np.ascontiguousarray(sub.transpose(1, 0, 2)).reshape(128, -1)

    w_in = np.asarray(inp["ab_w_in"][0], np.float32)
    put("in_cq0", kmaj(w_in, np.arange(0, 128)))
    put("in_cq1", kmaj(w_in, np.arange(128, 256)))
    put("in_ckv", kmaj(w_in, np.arange(256, 384)))
    wpad = np.zeros((1024, 128), np.float32)
    wpad[:, 64:96] = w_in[:, 384:416]
    put("in_kpe", kmaj(wpad, np.arange(128)))
    wpad = np.zeros((1024, 128), np.float32)
    wpad[:, 64:80] = w_in[:, 400:416]
    wpad[:, 80:96] = w_in[:, 384:400]
    put("in_kpr", kmaj(wpad, np.arange(128)))
    for c in range(4):
        put("in_xl%d" % c, kmaj(w_in, np.arange(416 + c * 128, 416 + (c + 1) * 128)))
        put("in_gt%d" % c, kmaj(w_in, np.arange(928 + c * 128, 928 + (c + 1) * 128)))
    wq = np.asarray(inp["ab_w_q_b"][0], np.float32)
    for h in range(8):
        wp_ = np.zeros((256, 128), np.float32)
        wp_[:, 0:96] = wq[:, h * 96:(h + 1) * 96]
        put("wq%d" % h, kmaj(wp_, np.arange(128)))
        wr = np.zeros((256, 128), np.float32)
        wr[:, 64:80] = wq[:, h * 96 + 80:h * 96 + 96]
        wr[:, 80:96] = wq[:, h * 96 + 64:h * 96 + 80]
        put("wqr%d" % h, kmaj(wr, np.arange(128)))
    wkv = np.asarray(inp["ab_w_kv_b"][0], np.float32)
    for h in range(8):
        wk_ = np.zeros((128, 128), np.float32)
        wk_[:, 0:64] = wkv[:, h * 128:h * 128 + 64]
        put("wk%d" % h, wk_)
    put("wv", np.concatenate([wkv[:, h * 128 + 64:h * 128 + 128] for h in range(8)], axis=1))
    wa = np.asarray(inp["ab_w_rg_a"][0], np.float32)
    wx = np.asarray(inp["ab_w_rg_x"][0], np.float32)
    for c in range(4):
        for nm, w in (("wa", wa), ("wx", wx)):
            bd = np.zeros((128, 128), np.float32)
            bd[0:64, 0:64] = w[2 * c]
            bd[64:128, 64:128] = w[2 * c + 1]
            put("%s%d" % (nm, c), bd)
    wo = np.asarray(inp["ab_w_out"][0], np.float32)
    for c in range(4):
        put("wom%d" % c, np.ascontiguousarray(wo[c * 128:(c + 1) * 128]))
    for c in range(4):
        put("wol%d" % c, np.ascontiguousarray(wo[512 + c * 128:512 + (c + 1) * 128]))
    cin = np.asarray(inp["c_w_in"][0], np.float32)
    for uc in range(8):
        put("cu%d" % uc, kmaj(cin, np.arange(uc * 128, (uc + 1) * 128)))
    for hf in range(2):
        put("cv%d" % hf, kmaj(cin, np.arange(1024 + hf * 512, 1024 + (hf + 1) * 512)))
    cout = np.asarray(inp["c_w_out"][0], np.float32)
    for oc in range(8):
        put("co%d" % oc, kmaj(cout, np.arange(oc * 128, (oc + 1) * 128)))
    for l in range(2):
        wg = np.asarray(inp["ffn_w_gate"][l], np.float32)
        wu = np.asarray(inp["ffn_w_up"][l], np.float32)
        wd = np.asarray(inp["ffn_w_down"][l], np.float32)
        for fc in range(NFC):
            put("fg%d_%d" % (l, fc), kmaj(wg, np.arange(fc * 128, (fc + 1) * 128)))
            put("fu%d_%d" % (l, fc), kmaj(wu, np.arange(fc * 128, (fc + 1) * 128)))
        for oc in range(8):
            put("fd%d_%d" % (l, oc), kmaj(wd, np.arange(oc * 128, (oc + 1) * 128)))
    return W


def isap(x):
    return hasattr(x, "tensor")


class Builder:
    def __init__(self, NSEQ, S, dbg=0):
        self.dbg = dbg
        self.NSEQ, self.S = NSEQ, S
        self.NBLK = S // TB
        nc = self.nc = bass.Bass("TRN2", target_bir_lowering=False)
        self.P = Prog(nc)
        self.units, self.woff, self.pieces, self.WTOT = weight_plan()
        self.poff, self.NPRM = param_plan()
        dt = nc.dram_tensor
        self.x_d = dt("x", [NSEQ, S, D], F32, kind="ExternalInput").ap()
        self.pos_d = dt("posr", [NSEQ, 32, S], I32, kind="ExternalInput").ap()
        self.wst_d = dt("wst", [128, self.WTOT], F32, kind="ExternalInput").ap()
        self.prm_d = dt("prm", [128, self.NPRM], F32, kind="ExternalInput").ap()
        self.rowp_d = dt("rowp", [128, 1024], F32, kind="ExternalInput").ap()
        self.wsT_d = dt("wsT", [128, 1024], F32, kind="ExternalInput").ap()
        self.wbf_d = dt("wbf", [128, self.WTOT], BF16, kind="Internal").ap()
        self.out_d = dt("out", [NSEQ, S, D], F32, kind="ExternalOutput").ap()
        if dbg:
            self.dbg_h = dt("dbg_h", [128, 8 * TB], F32, kind="ExternalOutput").ap()
            self.dbg_b = dt("dbg_b", [128, 24 * TB], BF16, kind="ExternalOutput").ap()
            self.dbg_k = dt("dbg_k", [128, 8 * S], BF16, kind="ExternalOutput").ap()
            self.dbg_v = dt("dbg_v", [128, (S // 128) * 512], BF16, kind="ExternalOutput").ap()
            self.dbg_n = dt("dbg_n", [128, 8 * TB], BF16, kind="ExternalOutput").ap()
        sb = nc.alloc_sbuf_tensor
        self.hres = sb("hres", [128, 8 * TB], F32)
        self.kc = sb("kc", [128, 8 * S], BF16)
        self.vc = sb("vc", [128, (S // 128) * 512], BF16)
        self.ident = sb("ident", [128, 128], F32)
        self.ones_bf = sb("ones_bf", [128, 128], BF16)
        self.ones_f = sb("ones_f", [128, 128], F32)
        self.tri = sb("tri", [128, 128], BF16)
        self.wsTb = sb("wsTb", [128, 1024], BF16)
        self.rowp = sb("rowp_sb", [128, 1024], F32)
        self.prm = sb("prm_sb", [128, self.NPRM], F32)
        self.cneg = sb("cneg", [128, 4], F32)
        self.lst = sb("lst", [128, 4], F32)
        self.xlh = sb("xlh", [128, 12], F32)
        self.ghalo = sb("ghalo", [128, 2 * NFC * 2], F32)
        self.small = sb("small", [128, 96], F32)
        self.lruc = sb("lruc", [128, 12], F32)
        self.lruC = sb("lruC", [128, 4 * TB], F32)
        self.hn = sb("hn", [128, 8 * TB], BF16)
        self.bigb = sb("bigb", [128, 24 * TB], BF16)
        self.nrm = sb("nrm", [128, 3 * TB], BF16)
        self.fbuf = sb("fbuf", [128, 4096], F32)
        self.post = sb("post", [128, TB], I32)
        self.tmp = [sb("tmp%d" % i, [128, TW], F32) for i in range(NTMP)]
        self.pT = [sb("pT%d" % i, [128, TB], BF16) for i in range(4)]
        self.ring = sb("ring", [128, RING_COLS], BF16)
        self.cs = sb("cs", [128, 2 * TB], F32)
        self.gg = sb("gg", [128, 4 * TB], F32)
        self.xcp = sb("xcp", [128, 4 * TB], F32)
        self.rsb = sb("rsb", [128, TB], F32)
        self.banks = [nc.alloc_psum_tensor("bank%d" % i, [128, 512], F32) for i in range(8)]
        self._tmp_i = 0
        self._bank_i = 0
        self._pT_i = 0
        self.ring_pos = 0
        self.resident = {}

    def T(self):
        t = self.tmp[self._tmp_i % NTMP]
        self._tmp_i += 1
        return t

    def bank(self, pool=None):
        pool = pool or range(8)
        pool = list(pool)
        b = self.banks[pool[self._bank_i % len(pool)]]
        self._bank_i += 1
        return b

    def prmc(self, name, i=0, rows=slice(0, 128)):
        o = self.poff[name] + i
        return self.prm[rows, o:o + 1]

    def TT(self, eng, out, a, b, op):
        self.P._record(eng, lambda e, out=out, a=a, b=b, op=op: e.tensor_tensor(out=out, in0=a, in1=b, op=op), [a, b], [out])

    def TS(self, eng, out, a, s1, s2, op0, op1=None):
        rd = [a] + [s for s in (s1, s2) if isap(s)]
        if op1 is None:
            self.P._record(eng, lambda e, out=out, a=a, s1=s1, op0=op0: e.tensor_scalar(out=out, in0=a, scalar1=s1, scalar2=None, op0=op0), rd, [out])
        else:
            self.P._record(eng, lambda e, out=out, a=a, s1=s1, s2=s2, op0=op0, op1=op1: e.tensor_scalar(out=out, in0=a, scalar1=s1, scalar2=s2, op0=op0, op1=op1), rd, [out])

    def STT(self, out, a, s, b, op0, op1):
        rd = [a, b] + ([s] if isap(s) else [])
        self.P._record("dve", lambda e, out=out, a=a, s=s, b=b, op0=op0, op1=op1: e.scalar_tensor_tensor(out=out, in0=a, scalar=s, in1=b, op0=op0, op1=op1), rd, [out])

    def ACT(self, out, a, func, bias=None, scale=None):
        rd = [a] + [s for s in (bias, scale) if isap(s)]
        kw = {}
        if bias is not None:
            kw["bias"] = bias
        if scale is not None:
            kw["scale"] = scale
        self.P._record("act", lambda e, out=out, a=a, func=func, kw=kw: e.activation(out=out, in_=a, func=func, **kw), rd, [out])

    def CP(self, eng, out, a):
        if eng == "act":
            self.P._record("act", lambda e, out=out, a=a: e.copy(out=out, in_=a), [a], [out])
        else:
            self.P._record(eng, lambda e, out=out, a=a: e.tensor_copy(out=out, in_=a), [a], [out])

    def MM(self, out, lhsT, rhs, start=True, stop=True):
        self.P._record("pe", lambda e, out=out, lhsT=lhsT, rhs=rhs, start=start, stop=stop: e.matmul(out, lhsT=lhsT, rhs=rhs, start=start, stop=stop), [lhsT, rhs], [out], tag=getattr(self, "phase", ""))

    def TR(self, out, a):
        self.P._record("pe", lambda e, out=out, a=a: e.transpose(out, a, self.ident[:]), [a, self.ident[:]], [out], tag=getattr(self, "phase", ""))

    def MEMSET(self, eng, ap, v):
        self.P._record(eng, lambda e, ap=ap, v=v: e.memset(ap, v), [], [ap])

    def DMA(self, q, out, in_):
        self.P._record(q, lambda e, out=out, in_=in_: e.dma_start(out=out, in_=in_), [in_], [out], is_dma=True)

    def W(self, name):
        o, c, pi = self.woff[name]
        pstart, pcols, _ = self.pieces[pi]
        if pi not in self.resident:
            if self.ring_pos + pcols > RING_COLS:
                self.ring_pos = 0
            base = self.ring_pos
            self.ring_pos += pcols
            for k in list(self.resident):
                kb = self.resident[k]
                kc_ = self.pieces[k][1]
                if kb < base + pcols and base < kb + kc_:
                    del self.resident[k]
            self.resident[pi] = base
            self.DMA("sp", self.ring[:, base:base + pcols], self.wbf_d[:, pstart:pstart + pcols])
        return self.resident[pi] + (o - pstart)

    def Wap(self, name, kc, M, rows=slice(0, 128), m0=0, m1=None):
        base = self.W(name)
        m1 = M if m1 is None else m1
        return self.ring[rows, base + kc * M + m0: base + kc * M + m1]

    def rmsnorm(self, src, nch, gname, dst, inv_n, dst_is_src=False):
        sq = self._sq
        self.rmsnorm_a(src, nch, sq)
        self.rmsnorm_b(src, nch, gname, dst, inv_n, sq)

    def rmsnorm_a(self, src, nch, sq):
        for c in range(nch):
            self.ACT(sq(c), src(c), AF.Square)

    def rmsnorm_b(self, src, nch, gname, dst, inv_n, sq):
        self._sq = sq
        ps = self.bank()
        for c in range(nch):
            self.MM(ps[:, :], self.ones_bf[:, :], self.sqb(dst, c), start=(c == 0), stop=(c == nch - 1))
        sd = self.T()
        self.ACT(sd[:, 0:TB], ps[:, :], AF.Sqrt, bias=self.eps_ap, scale=inv_n)
        rs = self.T()
        self.P._record("dve", lambda e, o=rs[:, 0:TB], a=sd[:, 0:TB]: e.reciprocal(out=o, in_=a), [sd[:, 0:TB]], [rs[:, 0:TB]])
        for c in range(nch):
            self.STT(dst(c), src(c), self.prmc(gname, c), rs[:, 0:TB], ALU.mult, ALU.mult)

    def sqb(self, dst, c):
        return self._sq(c)

    def build(self):
        P, nc = self.P, self.nc
        S, NSEQ = self.S, self.NSEQ
        hres, hn, bigb = self.hres, self.hn, self.bigb
        H = lambda c: hres[:, c * TB:(c + 1) * TB]
        HN = lambda c: hn[:, c * TB:(c + 1) * TB]
        BB = lambda c, rows=slice(0, 128): bigb[rows, c * TB:(c + 1) * TB]
        self.eps_t = nc.alloc_sbuf_tensor("eps_t", [128, 1], F32)
        self.eps_ap = self.eps_t[:, 0:1]
        self.MEMSET("pool", self.eps_t[:, :], EPS)
        self.qtr_t = nc.alloc_sbuf_tensor("qtr_t", [128, 1], F32)
        self.qtr_ap = self.qtr_t[:, 0:1]
        self.MEMSET("pool", self.qtr_t[:, :], 0.25)
        self.mhalf_t = nc.alloc_sbuf_tensor("mhalf_t", [128, 1], F32)
        self.mhalf_ap = self.mhalf_t[:, 0:1]
        self.MEMSET("pool", self.mhalf_t[:, :], -0.5)
        self.MEMSET("pool", self.ones_bf[:, :], 1.0)
        self.MEMSET("pool", self.ones_f[:, :], 1.0)
        P._record("pool", lambda e: e.affine_select(out=self.ident[:, :], in_=self.ones_f[:, :], pattern=[[-1, 128]], compare_op=ALU.is_equal, fill=0.0, base=0, channel_multiplier=1), [self.ones_f[:, :]], [self.ident[:, :]])
        P._record("pool", lambda e: e.affine_select(out=self.tri[:, :], in_=self.ones_f[:, :], pattern=[[1, 128]], compare_op=ALU.is_ge, fill=0.0, base=0, channel_multiplier=-1), [self.ones_f[:, :]], [self.tri[:, :]])
        self.DMA("sp", self.fbuf[:, 0:1024], self.wsT_d)
        self.DMA("sp", self.prm[:, :], self.prm_d)
        self.DMA("sp", self.rowp[:, :], self.rowp_d)
        for g in range(8):
            P._record("pool", lambda e, g=g: e.affine_select(out=self.wsTb[:, g * 128:(g + 1) * 128], in_=self.fbuf[:, g * 128:(g + 1) * 128], pattern=[[1, 128]], compare_op=ALU.is_ge, fill=0.0, base=0, channel_multiplier=-1), [self.fbuf[:, g * 128:(g + 1) * 128]], [self.wsTb[:, g * 128:(g + 1) * 128]])
        for half in range(2):
            ps = self.bank()
            for j in range(4):
                g = half * 4 + j
                self.MM(ps[:, j * 128:(j + 1) * 128], self.ones_bf[:, :], self.wsTb[:, g * 128:(g + 1) * 128])
            for j in range(4):
                g = half * 4 + j
                self.STT(self.rowp[:, g * 128:(g + 1) * 128], ps[:, j * 128:(j + 1) * 128], self.prmc("ln_b", g), self.rowp[:, g * 128:(g + 1) * 128], ALU.mult, ALU.add)
        lam = self.prm[:, self.poff["lam"]:self.poff["lam"] + 4]
        self.ACT(self.small[:, 0:4], lam, AF.Exp, scale=-1.0)
        self.ACT(self.small[:, 4:8], self.small[:, 0:4], AF.Ln, bias=self.ones_f[:, 0:1])
        self.TS("dve", self.cneg[:, :], self.small[:, 4:8], -8.0, None, ALU.mult)
        self.TS("dve", self.lruc[:, 0:4], self.prm[:, self.poff["b_a"]:self.poff["b_a"] + 4], 0.5, None, ALU.mult)
        self.TS("dve", self.lruc[:, 4:8], self.prm[:, self.poff["b_x"]:self.poff["b_x"] + 4], 0.5, None, ALU.mult)
        self.TS("dve", self.lruc[:, 8:12], self.small[:, 4:8], -4.0, None, ALU.mult)
        P._record("dve", lambda e: e.memzero(self.bigb[:, 0:(24 if self.dbg else 8) * TB]), [], [self.bigb[:, 0:(24 if self.dbg else 8) * TB]])
        P._record("dve", lambda e: e.memzero(self.kc[:, :]), [], [self.kc[:, :]])
        if self.dbg:
            self.MEMSET("pool", self.vc[:, :], 0.0)
            self.MEMSET("pool", self.hn[:, :], 0.0)
        self.MEMSET("pool", self.lst[:, :], 0.0)
        self.MEMSET("pool", self.xlh[:, :], 0.0)
        self.MEMSET("pool", self.ghalo[:, :], 0.0)
        CH = 2048
        for a in range(0, self.WTOT, CH):
            b = min(a + CH, self.WTOT)
            self.DMA("pool", self.wbf_d[:, a:b], self.wst_d[:, a:b])
        for s in range(NSEQ):
            if s > 0:
                self.MEMSET("pool", self.lst[:, :], 0.0)
                self.MEMSET("pool", self.xlh[:, :], 0.0)
                self.MEMSET("pool", self.ghalo[:, :], 0.0)
            for b in range(self.NBLK):
                if s == 0 and b == 0:
                    self.load_x(0, 0)
                self.block(s, b)
        P.emit()
        return nc

    def XS(self, tt):
        t = self.gg if tt < 2 else self.xcp
        return t[:, (tt % 2) * 1024:(tt % 2 + 1) * 1024]

    def load_x(self, s, b):
        t0 = b * TB
        for tt in range(4):
            self.DMA("sp", self.XS(tt), self.x_d[s, t0 + tt * 128:t0 + (tt + 1) * 128, :])
        R96 = slice(64, 96)
        self.DMA("sp", self.post[R96, :], self.pos_d[s, :, t0:t0 + TB])
        yv = self.T(); k1 = self.T(); k2 = self.T()
        sin_t = self.cs[:, 0:TB]; cos_t = self.cs[:, TB:2 * TB]
        V = lambda t: t[R96, 0:TB]
        self.TS("dve", V(yv), self.post[R96, :], self.prmc("invf", 0, R96), None, ALU.mult)
        self.TS("dve", V(k1), V(yv), MAGIC, None, ALU.add)
        self.TS("dve", V(k1), V(k1), MAGIC, None, ALU.subtract)
        self.TT("dve", V(k1), V(yv), V(k1), ALU.subtract)
        self.ACT(V(sin_t), V(k1), AF.Sin, scale=self.prmc("sgn2pi", 0, R96))
        self.TS("dve", V(yv), V(yv), 0.25, None, ALU.add)
        self.TS("dve", V(k2), V(yv), MAGIC, None, ALU.add)
        self.TS("dve", V(k2), V(k2), MAGIC, None, ALU.subtract)
        self.TT("dve", V(k2), V(yv), V(k2), ALU.subtract)
        self.ACT(V(cos_t), V(k2), AF.Sin, scale=6.2831845)

    def block(self, s, b):
        P = self.P
        self.PL = "dve" if (s == 0 and b == 0) else "pool"
        S = self.S
        t0 = b * TB
        hres, hn, bigb, fbuf = self.hres, self.hn, self.bigb, self.fbuf
        H = lambda c: hres[:, c * TB:(c + 1) * TB]
        HN = lambda c: hn[:, c * TB:(c + 1) * TB]
        BB = lambda c, rows=slice(0, 128): bigb[rows, c * TB:(c + 1) * TB]
        R96 = slice(64, 96)
        hres3 = hres[:, :].rearrange("p (c t) -> p c t", t=TB)
        self.phase = 'xin'
        for tt in range(4):
            stg = self.XS(tt)
            for half in range(2):
                ps = self.bank()
                for j in range(4):
                    c = half * 4 + j
                    self.TR(ps[:, j * 128:(j + 1) * 128], stg[:, c * 128:(c + 1) * 128])
                self.CP("act" if half == 0 else "dve", hres3[:, half * 4:half * 4 + 4, tt * 128:(tt + 1) * 128],
                        ps[:, :].rearrange("p (c t) -> p c t", t=128))
        sin_t = self.cs[:, 0:TB]; cos_t = self.cs[:, TB:2 * TB]
        V = lambda t: t[R96, 0:TB]
        self.phase = 'zproj'
        rs = self.rsb[:, 0:TB]
        for c in range(8):
            if c % 2 == 0:
                self.TS("dve", HN(c), H(c), self.prmc("ab_norm", c), None, ALU.mult)
            else:
                self.ACT(HN(c), H(c), AF.Copy, scale=self.prmc("ab_norm", c))
        for c in range(8):
            if c % 2 == 0:
                self.ACT(BB(8 + c), H(c), AF.Square)
            else:
                self.TT(self.PL, BB(8 + c), H(c), H(c), ALU.mult)

        def zproj(name, M, ps):
            for kc in range(8):
                self.MM(ps[0:M, :], self.Wap(name, kc, M), HN(kc), start=(kc == 0), stop=(kc == 7))

        CQ = lambda c: fbuf[:, 2048 + c * TB:2048 + (c + 1) * TB]
        CKV = fbuf[:, 2048 + 2 * TB:2048 + 3 * TB]
        for c in range(2):
            ps = self.bank()
            zproj("in_cq%d" % c, 128, ps)
            if c == 0:
                pss = self.bank()
                for c_ in range(8):
                    self.MM(pss[:, :], self.ones_bf[:, :], BB(8 + c_), start=(c_ == 0), stop=(c_ == 7))
                sd = self.T()
                self.ACT(sd[:, 0:TB], pss[:, :], AF.Sqrt, bias=self.eps_ap, scale=1.0 / 1024)
                P._record("dve", lambda e, o=rs, a=sd[:, 0:TB]: e.reciprocal(out=o, in_=a), [sd[:, 0:TB]], [rs])
            self.TT("dve", CQ(c), ps[:, :], rs, ALU.mult)
        ps = self.bank()
        zproj("in_ckv", 128, ps)
        self.TT("dve", CKV, ps[:, :], rs, ALU.mult)
        ps_kpe = self.bank()
        zproj("in_kpe", 128, ps_kpe)
        ps_kpr = self.bank()
        zproj("in_kpr", 128, ps_kpr)
        t1 = self.T(); t2 = self.T()
        self.TT("dve", V(t1), ps_kpe[R96, :], V(cos_t), ALU.mult)
        self.TT("dve", V(t2), ps_kpr[R96, :], V(sin_t), ALU.mult)
        kpb = self.pT[3]
        self.TT("dve", V(t1), V(t1), V(t2), ALU.add)
        self.TT("dve", kpb[R96, :], V(t1), rs[R96, :], ALU.mult)
        for h in range(8):
            self.CP("act" if h % 2 == 0 else self.PL, self.kc[R96, h * S + t0:h * S + t0 + TB], kpb[R96, :])
        self.phase = 'qkvnorm'
        NR = lambda c: self.nrm[:, c * TB:(c + 1) * TB]
        sq_q = lambda c: NR(c)
        sq_kv = lambda c: NR(2)
        self.rmsnorm_a(CQ, 2, sq_q)
        self.rmsnorm_a(lambda c: CKV, 1, sq_kv)
        self.phase = 'lru'
        xlts = []
        for c in range(4):
            ps_x = self.bank()
            zproj("in_xl%d" % c, 128, ps_x)
            ps_g = self.bank()
            zproj("in_gt%d" % c, 128, ps_g)
            if c == 0:
                self.phase = 'qkvnorm'
                self.rmsnorm_b(CQ, 2, "q_norm", NR, 1.0 / 256, sq_q)
                self.rmsnorm_b(lambda c_: CKV, 1, "kv_norm", lambda c_: NR(2), 1.0 / 128, sq_kv)
                self.phase = 'lru'
            xlt = self.T()
            xlts.append(xlt)
            self.CP(self.PL, xlt[:, 0:3], self.xlh[:, c * 3:c * 3 + 3])
            self.TT("dve", xlt[:, 3:3 + TB], ps_x[:, :], rs, ALU.mult)
            self.CP(self.PL, self.xlh[:, c * 3:c * 3 + 3], xlt[:, TB:TB + 3])
            ggc = self.gg[:, c * TB:(c + 1) * TB]
            self.TT("dve", ggc, ps_g[:, :], rs, ALU.mult)
            self.ACT(ggc, ggc, GELU)
            xc = self.xcp[:, c * TB:(c + 1) * TB]
            self.ACT(xc, xlt[:, 0:TB], AF.Identity, bias=self.prmc("lconv_b", c), scale=self.prmc("lconv_w", c * 4))

        def lru_steps(c):
            xc = self.xcp[:, c * TB:(c + 1) * TB]
            gg = self.gg[:, c * TB:(c + 1) * TB]
            xcb = BB(20 + c)
            gbk = self.banks[3]
            A = fbuf[:, c * TB:(c + 1) * TB]
            Bt = fbuf[:, (4 + c) * TB:(5 + c) * TB]
            C = self.lruC[:, c * TB:(c + 1) * TB]
            Hs = Bt

            def s1_():
                ba = self.W("wa%d" % c)
                self.MM(gbk[:, :], self.ring[:, ba:ba + 128], xcb)
                self.ACT(A, gbk[:, :], AF.Tanh, bias=self.lruc[:, c:c + 1], scale=0.5)

            def s2_():
                bx_ = self.W("wx%d" % c)
                self.MM(gbk[:, :], self.ring[:, bx_:bx_ + 128], xcb)
                self.ACT(C, gbk[:, :], AF.Tanh, bias=self.lruc[:, 4 + c:5 + c], scale=0.5)

            def s3_():
                self.ACT(A, A, AF.Exp, bias=self.lruc[:, 8 + c:9 + c], scale=self.lruc[:, 8 + c:9 + c])

            def s4_():
                self.ACT(Bt, A, AF.Square)

            def s5_():
                self.ACT(Bt, Bt, AF.Sqrt, bias=self.qtr_ap, scale=-0.25)

            def s6_():
                self.STT(C, C, 1.0, xc, ALU.add, ALU.mult)
                self.TT("dve", C, C, Bt, ALU.mult)

            def s7_():
                P._record("dve", lambda e, o_=Hs, a=A, bb=C, ini=self.lst[:, c:c + 1]: e.tensor_tensor_scan(out=o_, data0=a, data1=bb, initial=ini, op0=ALU.mult, op1=ALU.add),
                          [A, C, self.lst[:, c:c + 1]], [Hs])
                self.CP(self.PL, self.lst[:, c:c + 1], Hs[:, TB - 1:TB])

            def s8_():
                self.TT("dve", BB(16 + c), Hs, gg, ALU.mult)

            return [s1_, s2_, s3_, s4_, s5_, s6_, s7_, s8_]

        self.phase = 'qheads'
        for h in range(8):
            psq = self.bank()
            psr = self.bank()
            for kc in range(2):
                self.MM(psq[:, :], self.Wap("wq%d" % h, kc, 128), NR(kc), start=(kc == 0), stop=(kc == 1))
            for kc in range(2):
                self.MM(psr[:, :], self.Wap("wqr%d" % h, kc, 128), NR(kc), start=(kc == 0), stop=(kc == 1))
            self.CP("act", BB(h, slice(0, 64)), psq[0:64, :])
            a1 = self.lruC[:, (h % 2) * 2 * TB:(h % 2) * 2 * TB + TB]
            a2 = self.lruC[:, (h % 2) * 2 * TB + TB:(h % 2) * 2 * TB + 2 * TB]
            self.TT("dve", a1[R96, :], psq[R96, :], V(cos_t), ALU.mult)
            self.TT("dve", a2[R96, :], psr[R96, :], V(sin_t), ALU.mult)
            self.TT(self.PL, BB(h, R96), a1[R96, :], a2[R96, :], ALU.add)
        self.phase = 'kv'
        for h in range(8):
            ps = self.bank()
            base = self.W("wk%d" % h)
            self.MM(ps[:, :], self.ring[:, base:base + 128], NR(2))
            self.CP("act", self.kc[0:64, h * S + t0:h * S + t0 + TB], ps[0:64, :])
        basev = self.W("wv")
        for tt in range(4):
            ps = self.bank()
            self.MM(ps[:, :], NR(2)[:, tt * 128:(tt + 1) * 128], self.ring[:, basev:basev + 512])
            vt = (t0 // 128 + tt) * 512
            self.CP("dve", self.vc[:, vt:vt + 512], ps[:, :])
        def conv_step(c):
            def f():
                xlt = xlts[c]
                xc = self.xcp[:, c * TB:(c + 1) * TB]
                for k in range(1, 4):
                    self.STT(xc, xlt[:, k:k + TB], self.prmc("lconv_w", c * 4 + k), xc, ALU.mult, ALU.add)
                self.CP("act", BB(20 + c), xc)
            return f
        self.phase = 'attn'
        nkt = 4 * b + 4
        items = [(h, kt) for h in range(8) for kt in range(nkt)]
        LA = 2
        pend = {}

        def issue_s(i):
            h, kt = items[i]
            j = kt - 4 * b
            c0 = max(j, 0) * 128
            n = TB - c0
            s_ps = self.banks[i % 3]
            self.MM(s_ps[:, 0:n], self.kc[:, h * S + kt * 128:h * S + (kt + 1) * 128], BB(h)[:, c0:TB])
            pT = self.pT[i % 4]
            self.ACT(pT[:, 0:n], s_ps[:, 0:n], AF.Exp, scale=SCALE)
            if j >= 0:
                self.TT(self.PL, pT[:, 0:128], pT[:, 0:128], self.tri[:, :], ALU.mult)
            pend[i] = (pT, c0, n)

        def issue_pv(i):
            h, kt = items[i]
            pT, c0, n = pend.pop(i)
            hp = h // 2
            o_ps = self.banks[4 + (h % 2)]
            d_ps = self.banks[6 + (h % 2)]
            self.MM(o_ps[:, c0:TB], self.vc[:, kt * 512 + hp * 128:kt * 512 + (hp + 1) * 128], pT[:, 0:n], start=(kt == 0), stop=(kt == nkt - 1))
            self.MM(d_ps[:, c0:TB], self.ones_bf[:, :], pT[:, 0:n], start=(kt == 0), stop=(kt == nkt - 1))
            if kt == nkt - 1:
                rs = slice((h % 2) * 64, (h % 2) * 64 + 64)
                rd = self.T()
                P._record("dve", lambda e, o=rd[rs, 0:TB], a=d_ps[rs, :]: e.reciprocal(out=o, in_=a), [d_ps[rs, :]], [rd[rs, 0:TB]])
                self.TT("dve", BB(8 + hp, rs), o_ps[rs, :], rd[rs, 0:TB], ALU.mult)

        NI = len(items)
        sched = {}
        for c in range(4):
            sched.setdefault(c, []).append(conv_step(c))
        steps = [lru_steps(c) for c in range(4)]
        last = NI + LA - 1
        q4 = NI // 4
        for c in range(4):
            for k in range(4):
                sched.setdefault(min(last, c * q4 + 5 + k), []).append(steps[c][k])
        burst = min(last, 3 * q4 + 9)
        for c in range(4):
            sched.setdefault(burst, []).append(steps[c][4])
        rest = max(1, (last - burst) // 13)
        pos = burst
        for c in range(4):
            for k in range(5, 8):
                pos = min(last, pos + rest)
                sched.setdefault(pos, []).append(steps[c][k])
        for i in range(NI + LA):
            if i < NI:
                issue_s(i)
            if i - LA >= 0:
                issue_pv(i - LA)
            for fn in sched.get(i, []):
                ph = self.phase
                self.phase = 'lru2'
                fn()
                self.phase = ph
        self.phase = 'oproj'
        for grp in range(2):
            pss = [self.banks[grp * 4 + j] for j in range(4)]
            srcs = [("wom%d" % c, BB(8 + c)) for c in range(4)] + [("wol%d" % c, BB(16 + c)) for c in range(4)]
            for ci, (wn, rhs) in enumerate(srcs):
                base = self.W(wn)
                for j in range(4):
                    oc = grp * 4 + j
                    self.MM(pss[j][:, :], self.ring[:, base + oc * 128:base + (oc + 1) * 128], rhs, start=(ci == 0), stop=(ci == 7))
            for j in range(4):
                oc = grp * 4 + j
                self.TT("dve", H(oc), H(oc), pss[j][:, :], ALU.add)
                self.TS("dve", HN(oc), H(oc), self.prmc("ffn_norm0", oc), None, ALU.mult)
        if self.dbg == 1:
            return self.dump()
        self.phase = 'ffn0'
        self.ffn(0)
        if self.dbg == 2:
            return self.dump()
        self.phase = 'sgu'
        self.rmsnorm_b(H, 8, "c_norm", HN, 1.0 / 1024, lambda c: HN(c))
        bias2 = self.rowp
        bigb3 = bigb[:, :].rearrange("p (c t) -> p c t", t=TB)
        VSB = lambda tt: fbuf[:, tt * 1024:(tt + 1) * 1024]
        VLN = lambda tt: bigb[:, (16 + 2 * tt) * TB:(16 + 2 * tt) * TB + 1024]
        def sgu_u():
            for uc in range(8):
                ps = self.bank()
                for kc in range(8):
                    self.MM(ps[:, :], self.Wap("cu%d" % uc, kc, 128), HN(kc), start=(kc == 0), stop=(kc == 7))
                self.ACT(BB(uc), ps[:, :], GELU)

        def sgu_v(tt):
            vsb = VSB(tt)
            for hf in range(2):
                ps = self.bank()
                for kc in range(8):
                    base = self.W("cv%d" % hf)
                    self.MM(ps[:, :], HN(kc)[:, tt * 128:(tt + 1) * 128], self.ring[:, base + kc * 512:base + (kc + 1) * 512], start=(kc == 0), stop=(kc == 7))
                self.ACT(vsb[:, hf * 512:(hf + 1) * 512], ps[:, :], GELU)
            st = self.small
            o = 8 + tt * 20
            for hf in range(2):
                P._record("dve", lambda e, o_=st[:, o + hf * 6:o + 6 + hf * 6], a=vsb[:, hf * 512:(hf + 1) * 512]: e.bn_stats(out=o_, in_=a), [vsb[:, hf * 512:(hf + 1) * 512]], [st[:, o + hf * 6:o + 6 + hf * 6]])
            P._record("dve", lambda e, o_=st[:, o + 12:o + 14], a=st[:, o:o + 12]: e.bn_aggr(out=o_, in_=a), [st[:, o:o + 12]], [st[:, o + 12:o + 14]])
            self.TS("dve", st[:, o + 14:o + 15], st[:, o + 13:o + 14], EPS, None, ALU.add)
            self.TT("pool", st[:, o + 15:o + 16], st[:, o + 14:o + 15], self.mhalf_ap, ALU.pow)
            self.TS("dve", VLN(tt), vsb, st[:, o + 12:o + 13], st[:, o + 15:o + 16], ALU.subtract, ALU.mult)

        def sgu_sp(tt):
            vln = VLN(tt)
            for half in range(2):
                ps = self.bank()
                for j in range(4):
                    g = half * 4 + j
                    self.MM(ps[:, j * 128:(j + 1) * 128], vln[:, g * 128:(g + 1) * 128], self.wsTb[:, g * 128:(g + 1) * 128])
                sm = self.T()
                for j in range(4):
                    g = half * 4 + j
                    self.STT(sm[:, j * 128:(j + 1) * 128], ps[:, j * 128:(j + 1) * 128], self.prmc("ln_g", g), bias2[:, g * 128:(g + 1) * 128], ALU.mult, ALU.add)
                self.TT("dve", bigb3[:, 8 + half * 4:8 + half * 4 + 4, tt * 128:(tt + 1) * 128],
                        sm[:, 0:TB].rearrange("p (c t) -> p c t", t=128),
                        bigb3[:, half * 4:half * 4 + 4, tt * 128:(tt + 1) * 128], ALU.mult)

        sgu_u()
        sgu_v(0)
        sgu_v(1)
        sgu_sp(0)
        sgu_v(2)
        sgu_sp(1)
        sgu_v(3)
        sgu_sp(2)
        sgu_sp(3)
        for oc in range(8):
            ps = self.bank()
            for g in range(8):
                self.MM(ps[:, :], self.Wap("co%d" % oc, g, 128), BB(8 + g), start=(g == 0), stop=(g == 7))
            self.TT("dve", H(oc), H(oc), ps[:, :], ALU.add)
            self.TS("dve", HN(oc), H(oc), self.prmc("ffn_norm1", oc), None, ALU.mult)
        if self.dbg == 3:
            return self.dump()
        self.phase = 'ffn1'
        nb_, ns_ = b + 1, s
        if nb_ == self.NBLK:
            nb_, ns_ = 0, s + 1
        if ns_ < self.NSEQ:
            self.load_x(ns_, nb_)
        self.ffn(1)
        if self.dbg == 4:
            return self.dump()
        self.phase = 'fin'
        self.rmsnorm_b(H, 8, "final_norm", H, 1.0 / 1024, lambda c: HN(c))
        for tt in range(4):
            stg = fbuf[:, (tt % 2) * 1024:(tt % 2 + 1) * 1024]
            for half in range(2):
                ps = self.bank()
                for j in range(4):
                    c = half * 4 + j
                    self.TR(ps[:, j * 128:(j + 1) * 128], H(c)[:, tt * 128:(tt + 1) * 128])
                self.CP("act" if half == 0 else "dve", stg[:, half * 512:(half + 1) * 512], ps[:, :])
            self.DMA("act", self.out_d[s, t0 + tt * 128:t0 + (tt + 1) * 128, :], stg)

    def dump(self):
        self.DMA("sp", self.dbg_h, self.hres[:, :])
        self.DMA("sp", self.dbg_b, self.bigb[:, :])
        self.DMA("sp", self.dbg_k[0:96, :], self.kc[0:96, :])
        self.DMA("sp", self.dbg_v, self.vc[:, :])
        self.DMA("sp", self.dbg_n, self.hn[:, :])

    def ffn(self, l):
        P = self.P
        hres, hn, bigb, fbuf = self.hres, self.hn, self.bigb, self.fbuf
        H = lambda c: hres[:, c * TB:(c + 1) * TB]
        HN = lambda c: hn[:, c * TB:(c + 1) * TB]
        BB = lambda c: bigb[:, c * TB:(c + 1) * TB]
        rs = self.rsb[:, 0:TB]
        for c in range(8):
            self.ACT(BB(14 + c), H(c), AF.Square)
        st = {}

        def stage_mm(fc):
            ps_g = self.bank()
            for kc in range(8):
                self.MM(ps_g[:, :], self.Wap("fg%d_%d" % (l, fc), kc, 128), HN(kc), start=(kc == 0), stop=(kc == 7))
            if fc == 0:
                pss = self.bank()
                for c in range(8):
                    self.MM(pss[:, :], self.ones_bf[:, :], BB(14 + c), start=(c == 0), stop=(c == 7))
                sd = self.T()
                self.ACT(sd[:, 0:TB], pss[:, :], AF.Sqrt, bias=self.eps_ap, scale=1.0 / 1024)
                P._record("dve", lambda e, o=rs, a=sd[:, 0:TB]: e.reciprocal(out=o, in_=a), [sd[:, 0:TB]], [rs])
            ps_u = self.bank()
            for kc in range(8):
                self.MM(ps_u[:, :], self.Wap("fu%d_%d" % (l, fc), kc, 128), HN(kc), start=(kc == 0), stop=(kc == 7))
            st[fc] = {"g": ps_g, "u": ps_u}

        def stage_ab(fc):
            d_ = st[fc]
            gb = fbuf[:, 2048 + (fc % 3) * 520:2048 + (fc % 3) * 520 + 514]
            gh = self.ghalo[:, (l * NFC + fc) * 2:(l * NFC + fc) * 2 + 2]
            self.CP(self.PL, gb[:, 0:2], gh)
            self.TT("dve", gb[:, 2:2 + TB], d_["g"][:, :], rs, ALU.mult)
            self.CP(self.PL, gh, gb[:, TB:TB + 2])
            t = self.T()
            self.ACT(t[:, 0:TB], gb[:, 0:TB], AF.Identity, bias=self.prmc("fconv_b%d" % l, fc), scale=self.prmc("fconv_w%d" % l, fc * 3))
            d_["gb"] = gb
            d_["t"] = t

        def stage_cd(fc):
            d_ = st[fc]
            gb, t = d_["gb"], d_["t"]
            self.STT(t[:, 0:TB], gb[:, 1:1 + TB], self.prmc("fconv_w%d" % l, fc * 3 + 1), t[:, 0:TB], ALU.mult, ALU.add)
            self.STT(t[:, 0:TB], gb[:, 2:2 + TB], self.prmc("fconv_w%d" % l, fc * 3 + 2), t[:, 0:TB], ALU.mult, ALU.add)
            ge = self.T()
            self.ACT(ge[:, 0:TB], t[:, 0:TB], GELU)
            d_["ge"] = ge

        def stage_e(fc):
            d_ = st.pop(fc)
            self.TT("dve", BB(fc), d_["ge"][:, 0:TB], d_["u"][:, :], ALU.mult)

        for i in range(NFC + 2):
            if i < NFC:
                stage_mm(i)
                stage_ab(i)
            if 0 <= i - 1 < NFC:
                stage_cd(i - 1)
            if 0 <= i - 2 < NFC:
                stage_e(i - 2)
        self.ACT(self.small[:, 90:91], self.eps_ap, AF.Sqrt)
        for oc in range(8):
            ps = self.bank()
            for fc in range(NFC):
                self.MM(ps[:, :], self.Wap("fd%d_%d" % (l, oc), fc, 128), BB(fc), start=(fc == 0), stop=(fc == NFC - 1))
            d = self.T()
            self.TT("dve", d[:, 0:TB], ps[:, :], rs, ALU.mult)
            self.TT(self.PL, H(oc), H(oc), d[:, 0:TB], ALU.add)
            self.ACT(HN(oc), H(oc), AF.Square)


_CACHE = {}


def get_program(NSEQ, S, dbg=0):
    key = (NSEQ, S, dbg)
    if key not in _CACHE:
        _CACHE[key] = Builder(NSEQ, S, dbg).build()
    return _CACHE[key]


def make_in_maps(inp, n_cores, NSEQ, S):
    x = np.asarray(inp["x"], np.float32)
    pos = np.asarray(inp["positions"], np.int32)
    wst = pack_weights(inp)
    prm = pack_params(inp)
    rowp = np.zeros((128, 1024), np.float32)
    rowp[:, :] = np.asarray(inp["c_b_s"][0], np.float32).reshape(1, 1024)
    ws = np.asarray(inp["c_w_s"][0], np.float32)
    wsT = np.ascontiguousarray(ws.transpose(2, 0, 1)).reshape(128, 1024)
    maps = []
    for i in range(n_cores):
        xs = np.ascontiguousarray(x[i * NSEQ:(i + 1) * NSEQ, :S])
        ps = pos[i * NSEQ:(i + 1) * NSEQ, :S]
        posr = np.ascontiguousarray(np.broadcast_to(ps[:, None, :], (NSEQ, 32, S))).astype(np.int32)
        maps.append({"x": xs, "posr": posr, "wst": wst, "prm": prm, "rowp": rowp, "wsT": wsT})
    return maps


def kernel(**inputs):
    n_cores = 8
    B, S, _ = inputs["x"].shape
    NSEQ = B // n_cores
    nc = get_program(NSEQ, S)
    maps = make_in_maps(inputs, n_cores, NSEQ, S)
    res = run_bass_kernel_spmd(nc, maps, core_ids=list(range(n_cores)))
    out = np.concatenate([np.asarray(r["out"], np.float32).reshape(NSEQ, S, D) for r in res.results], axis=0)
    return out
```

```python
import concourse.bass as bass
import concourse.mybir as mybir

F32 = mybir.dt.float32
BF16 = mybir.dt.bfloat16
I32 = mybir.dt.int32
AF = mybir.ActivationFunctionType
ALU = mybir.AluOpType

ENGS = ("pe", "act", "dve", "pool", "sp")


def _box(ap):
    t = ap.tensor
    name = t.name
    dims = ap.ap
    off = int(ap.offset)
    sp = str(ap.space)
    if "DRAM" in sp.upper() or "HBM" in sp.upper() or "dram" in sp:
        if len(dims) == 2 and dims[1][0] == 1 and dims[0][0] >= dims[1][1]:
            rs = dims[0][0]
            p0 = off // rs
            lo = off - p0 * rs
            return (name, p0, p0 + dims[0][1], lo, lo + dims[1][1])
        ext = 1
        for st, n in dims:
            ext += (n - 1) * abs(st)
        return (name, 0, 1 << 30, off, off + ext)
    pstep, pn = dims[0]
    if pstep == 0:
        pstep = 1 << 40
    p0 = off // pstep if pn > 1 or pstep < (1 << 40) else 0
    lo = off - p0 * pstep
    ext = 1
    for st, n in dims[1:]:
        ext += (n - 1) * abs(st)
    if "PSUM" in sp.upper():
        return (name, 0, 128, 0, 1 << 30)
    return (name, p0, p0 + pn, lo, lo + ext)


def _overlap(a, b):
    return a[1] < b[2] and b[1] < a[2] and a[3] < b[4] and b[3] < a[4]


def _covers(a, b):
    return a[1] <= b[1] and a[2] >= b[2] and a[3] <= b[3] and a[4] >= b[4]


class Op:
    __slots__ = ("id", "eng", "fn", "deps", "is_dma", "sig", "cnt", "sem", "val", "qidx", "tag")


class Prog:
    def __init__(self, nc, n_dma_sems=None):
        self.nc = nc
        self.ops = []
        self.eng_ops = {e: [] for e in ENGS}
        self.acc = {}
        self.n_dma_sems = n_dma_sems or {"sp": 24, "pool": 8, "act": 4}
        self.dma_count = {e: 0 for e in ENGS}

    def _record(self, eng, fn, reads, writes, is_dma=False, tag=None):
        op = Op()
        op.id = len(self.ops)
        op.eng = eng
        op.fn = fn
        op.deps = {}
        op.is_dma = is_dma
        op.sig = False
        op.cnt = None
        op.sem = None
        op.val = None
        op.qidx = None
        op.tag = tag
        if is_dma:
            op.qidx = self.dma_count[eng]
            self.dma_count[eng] += 1
        for ap in reads:
            bx = _box(ap)
            lst = self.acc.setdefault(bx[0], [])
            for rec in lst:
                if rec[1] and _overlap(rec[0], bx):
                    self._dep(op, rec[2], "RAW")
            if not is_dma:
                for rec in lst:
                    if (not rec[1]) and rec[0] == bx and rec[2].eng == eng and not rec[2].is_dma:
                        rec[2] = op
                        break
                else:
                    lst.append([bx, False, op])
            else:
                lst.append([bx, False, op])
        for ap in writes:
            bx = _box(ap)
            lst = self.acc.setdefault(bx[0], [])
            keep = []
            for rec in lst:
                if _overlap(rec[0], bx):
                    if rec[2] is not op:
                        self._dep(op, rec[2], "WAW" if rec[1] else "WAR")
                    if _covers(bx, rec[0]):
                        continue
                keep.append(rec)
            keep.append([bx, True, op])
            self.acc[bx[0]] = keep
        self.ops.append(op)
        self.eng_ops[eng].append(op)
        return op

    def _dep(self, op, prod, kind):
        if prod is op:
            return
        if (not prod.is_dma) and (not op.is_dma) and prod.eng == op.eng:
            if op.eng == "pe":
                return
        old = op.deps.get(prod.id)
        if old is None or kind == "RAW":
            op.deps[prod.id] = kind

    def pe(self, fn, reads, writes, tag=None):
        return self._record("pe", fn, reads, writes, tag=tag)

    def act(self, fn, reads, writes, tag=None):
        return self._record("act", fn, reads, writes, tag=tag)

    def dve(self, fn, reads, writes, tag=None):
        return self._record("dve", fn, reads, writes, tag=tag)

    def pool(self, fn, reads, writes, tag=None):
        return self._record("pool", fn, reads, writes, tag=tag)

    def dma(self, q, out, in_, **kw):
        return self._record(q, lambda e: e.dma_start(out=out, in_=in_, **kw), [in_], [out], is_dma=True)

    def emit(self, final_wait_all=True):
        nc = self.nc
        ops = self.ops
        for op in ops:
            for pid in op.deps:
                p = ops[pid]
                if not p.is_dma:
                    p.sig = True
        for e in ENGS:
            c = 0
            for op in self.eng_ops[e]:
                if not op.is_dma and op.sig:
                    c += 1
                    op.cnt = c
        from contextlib import ExitStack

        with ExitStack() as es:
            esem = {e: es.enter_context(nc.semaphore("sem_" + e)) for e in ENGS}
            dsem = {}
            for q, n in self.n_dma_sems.items():
                if self.dma_count[q] > 0:
                    dsem[q] = [es.enter_context(nc.semaphore("dsem_%s_%d" % (q, i))) for i in range(n)]
            for op in ops:
                if op.is_dma:
                    n = len(dsem[op.eng])
                    op.sem = dsem[op.eng][op.qidx % n]
                    op.val = 16 * (op.qidx // n + 1)
            block = es.enter_context(nc.Block())
            engobj = {"pe": "tensor", "act": "scalar", "dve": "vector", "pool": "gpsimd", "sp": "sync"}

            def make(ename):
                def body(eng):
                    waited = {}

                    def wait(sem, key, val):
                        if waited.get(key, 0) >= val:
                            return
                        eng.wait_ge(sem, val)
                        waited[key] = val

                    for op in self.eng_ops[ename]:
                        for pid, kind in op.deps.items():
                            p = ops[pid]
                            if p.is_dma:
                                wait(p.sem, ("d", p.sem.num), p.val)
                            else:
                                wait(esem[p.eng], ("e", p.eng), p.cnt)
                        if op.is_dma:
                            n = len(dsem[ename])
                            if op.qidx >= n:
                                wait(op.sem, ("d", op.sem.num), op.val - 16)
                            ins = op.fn(eng)
                            ins.then_inc(op.sem, 16)
                        else:
                            ins = op.fn(eng)
                            if op.sig:
                                ins.then_inc(esem[ename], 1)
                    if final_wait_all:
                        if ename in dsem:
                            n = len(dsem[ename])
                            tot = self.dma_count[ename]
                            for i in range(min(n, tot)):
                                cnt_i = len(range(i, tot, n))
                                wait(dsem[ename][i], ("d", dsem[ename][i].num), 16 * cnt_i)

                return body

            for e in ENGS:
                if self.eng_ops[e]:
                    getattr(block, engobj[e])(make(e))


import math
import numpy as np
from contextlib import ExitStack
from concourse.bass_utils import run_bass_kernel_spmd

TB = 512
D = 1024
DFF = 2816
NFC = 22
SCALE = 96 ** -0.5
EPS = 1e-6
MAGIC = 12582912.0
GELU = AF.Gelu_apprx_tanh
PIECE_MAX = 4096
RING_COLS = 4 * 4096
TW = 520
NTMP = 7


def weight_plan():
    u = []
    for n in ("cq0", "cq1", "ckv"):
        u.append(("in_" + n, 8 * 128))
    u.append(("in_kpe", 8 * 128))
    u.append(("in_kpr", 8 * 128))
    for c in range(4):
        u.append(("in_xl%d" % c, 8 * 128))
        u.append(("in_gt%d" % c, 8 * 128))
    for h in range(8):
        u.append(("wq%d" % h, 2 * 128))
        u.append(("wqr%d" % h, 2 * 128))
    for h in range(8):
        u.append(("wk%d" % h, 128))
    u.append(("wv", 512))
    for c in range(4):
        u.append(("wa%d" % c, 128))
        u.append(("wx%d" % c, 128))
    for c in range(4):
        u.append(("wom%d" % c, 1024))
    for c in range(4):
        u.append(("wol%d" % c, 1024))
    for l in range(2):
        if l == 1:
            for uc in range(8):
                u.append(("cu%d" % uc, 8 * 128))
            for hf in range(2):
                u.append(("cv%d" % hf, 8 * 512))
            for oc in range(8):
                u.append(("co%d" % oc, 8 * 128))
        for fc in range(NFC):
            u.append(("fg%d_%d" % (l, fc), 8 * 128))
            u.append(("fu%d_%d" % (l, fc), 8 * 128))
        for oc in range(8):
            u.append(("fd%d_%d" % (l, oc), NFC * 128))
    off = {}
    pieces = []
    cur = 0
    pstart, pcols, pun = 0, 0, []
    for name, c in u:
        if pcols + c > PIECE_MAX and pun:
            pieces.append((pstart, pcols, pun))
            pstart, pcols, pun = cur, 0, []
        off[name] = (cur, c, len(pieces))
        pun.append(name)
        pcols += c
        cur += c
    pieces.append((pstart, pcols, pun))
    return u, off, pieces, cur


def param_plan():
    names = [("ab_norm", 8), ("q_norm", 2), ("kv_norm", 1), ("lconv_w", 16), ("lconv_b", 4),
             ("b_a", 4), ("b_x", 4), ("lam", 4), ("c_norm", 8), ("ffn_norm0", 8), ("ffn_norm1", 8),
             ("fconv_w0", 66), ("fconv_w1", 66), ("fconv_b0", 22), ("fconv_b1", 22), ("final_norm", 8),
             ("invf", 1), ("sgn2pi", 1), ("ln_g", 8), ("ln_b", 8)]
    off = {}
    c = 0
    for n, k in names:
        off[n] = c
        c += k
    return off, c


def fm(v):
    v = np.asarray(v, np.float32)
    return np.ascontiguousarray(v.reshape(-1, 128).T)


def pack_params(inp):
    off, n = param_plan()
    p = np.zeros((128, n), np.float32)

    def put(name, arr):
        arr = np.asarray(arr, np.float32)
        p[:, off[name]:off[name] + arr.shape[1]] = arr

    put("ab_norm", fm(inp["ab_norm"][0]))
    put("q_norm", fm(inp["ab_q_norm"][0]))
    put("kv_norm", fm(inp["ab_kv_norm"][0]))
    cw = np.asarray(inp["ab_conv_w"][0], np.float32)
    lw = np.zeros((128, 16), np.float32)
    for c in range(4):
        for k in range(4):
            lw[:, c * 4 + k] = cw[k, c * 128:(c + 1) * 128]
    put("lconv_w", lw)
    put("lconv_b", fm(inp["ab_conv_b"][0]))
    put("b_a", fm(inp["ab_b_rg_a"][0]))
    put("b_x", fm(inp["ab_b_rg_x"][0]))
    put("lam", fm(inp["ab_lambda"][0]))
    put("c_norm", fm(inp["c_norm"][0]))
    put("ffn_norm0", fm(inp["ffn_norm"][0]))
    put("ffn_norm1", fm(inp["ffn_norm"][1]))
    for l in range(2):
        fw_ = np.asarray(inp["ffn_conv_w"][l], np.float32)
        a = np.zeros((128, 66), np.float32)
        for fc in range(NFC):
            for k in range(3):
                a[:, fc * 3 + k] = fw_[k, fc * 128:(fc + 1) * 128]
        put("fconv_w%d" % l, a)
        put("fconv_b%d" % l, fm(inp["ffn_conv_b"][l]))
    put("final_norm", fm(inp["final_norm"]))
    invf = np.exp(-math.log(10000.0) * np.arange(16, dtype=np.float32) / 16).astype(np.float32)
    iv = np.zeros((128, 1), np.float32)
    sg = np.zeros((128, 1), np.float32)
    twopi = np.float32(6.2831845)
    for r in range(32):
        iv[64 + r, 0] = invf[r % 16] / np.float32(2 * math.pi)
        sg[64 + r, 0] = -twopi if r < 16 else twopi
    put("invf", iv)
    put("sgn2pi", sg)
    put("ln_g", fm(inp["c_ln_g"][0]))
    put("ln_b", fm(inp["c_ln_b"][0]))
    return p


def pack_weights(inp):
    u, off, pieces, tot = weight_plan()
    W = np.zeros((128, tot), np.float32)

    def put(name, arr):
        o, c, _ = off[name]
        assert arr.shape == (128, c), (name, arr.shape, c)
        W[:, o:o + c] = arr

    def kmaj(w, cols):
        K = w.shape[0]
        sub = w[:, cols].reshape(K // 128, 128, len(cols))
        return np.ascontiguousarray(sub.transpose(1, 0, 2)).reshape(128, -1)

    w_in = np.asarray(inp["ab_w_in"][0], np.float32)
    put("in_cq0", kmaj(w_in, np.arange(0, 128)))
    put("in_cq1", kmaj(w_in, np.arange(128, 256)))
    put("in_ckv", kmaj(w_in, np.arange(256, 384)))
    wpad = np.zeros((1024, 128), np.float32)
    wpad[:, 64:96] = w_in[:, 384:416]
    put("in_kpe", kmaj(wpad, np.arange(128)))
    wpad = np.zeros((1024, 128), np.float32)
    wpad[:, 64:80] = w_in[:, 400:416]
    wpad[:, 80:96] = w_in[:, 384:400]
    put("in_kpr", kmaj(wpad, np.arange(128)))
    for c in range(4):
        put("in_xl%d" % c, kmaj(w_in, np.arange(416 + c * 128, 416 + (c + 1) * 128)))
        put("in_gt%d" % c, kmaj(w_in, np.arange(928 + c * 128, 928 + (c + 1) * 128)))
    wq = np.asarray(inp["ab_w_q_b"][0], np.float32)
    for h in range(8):
        wp_ = np.zeros((256, 128), np.float32)
        wp_[:, 0:96] = wq[:, h * 96:(h + 1) * 96]
        put("wq%d" % h, kmaj(wp_, np.arange(128)))
        wr = np.zeros((256, 128), np.float32)
        wr[:, 64:80] = wq[:, h * 96 + 80:h * 96 + 96]
        wr[:, 80:96] = wq[:, h * 96 + 64:h * 96 + 80]
        put("wqr%d" % h, kmaj(wr, np.arange(128)))
    wkv = np.asarray(inp["ab_w_kv_b"][0], np.float32)
    for h in range(8):
        wk_ = np.zeros((128, 128), np.float32)
        wk_[:, 0:64] = wkv[:, h * 128:h * 128 + 64]
        put("wk%d" % h, wk_)
    put("wv", np.concatenate([wkv[:, h * 128 + 64:h * 128 + 128] for h in range(8)], axis=1))
    wa = np.asarray(inp["ab_w_rg_a"][0], np.float32)
    wx = np.asarray(inp["ab_w_rg_x"][0], np.float32)
    for c in range(4):
        for nm, w in (("wa", wa), ("wx", wx)):
            bd = np.zeros((128, 128), np.float32)
            bd[0:64, 0:64] = w[2 * c]
            bd[64:128, 64:128] = w[2 * c + 1]
            put("%s%d" % (nm, c), bd)
    wo = np.asarray(inp["ab_w_out"][0], np.float32)
    for c in range(4):
        put("wom%d" % c, np.ascontiguousarray(wo[c * 128:(c + 1) * 128]))
    for c in range(4):
        put("wol%d" % c, np.ascontiguousarray(wo[512 + c * 128:512 + (c + 1) * 128]))
    cin = np.asarray(inp["c_w_in"][0], np.float32)
    for uc in range(8):
        put("cu%d" % uc, kmaj(cin, np.arange(uc * 128, (uc + 1) * 128)))
    for hf in range(2):
        put("cv%d" % hf, kmaj(cin, np.arange(1024 + hf * 512, 1024 + (hf + 1) * 512)))
    cout = np.asarray(inp["c_w_out"][0], np.float32)
    for oc in range(8):
        put("co%d" % oc, kmaj(cout, np.arange(oc * 128, (oc + 1) * 128)))
    for l in range(2):
        wg = np.asarray(inp["ffn_w_gate"][l], np.float32)
        wu = np.asarray(inp["ffn_w_up"][l], np.float32)
        wd = np.asarray(inp["ffn_w_down"][l], np.float32)
        for fc in range(NFC):
            put("fg%d_%d" % (l, fc), kmaj(wg, np.arange(fc * 128, (fc + 1) * 128)))
            put("fu%d_%d" % (l, fc), kmaj(wu, np.arange(fc * 128, (fc + 1) * 128)))
        for oc in range(8):
            put("fd%d_%d" % (l, oc), kmaj(wd, np.arange(oc * 128, (oc + 1) * 128)))
    return W


def isap(x):
    return hasattr(x, "tensor")


class Builder:
    def __init__(self, NSEQ, S, dbg=0):
        self.dbg = dbg
        self.NSEQ, self.S = NSEQ, S
        self.NBLK = S // TB
        nc = self.nc = bass.Bass("TRN2", target_bir_lowering=False)
        self.P = Prog(nc)
        self.units, self.woff, self.pieces, self.WTOT = weight_plan()
        self.poff, self.NPRM = param_plan()
        dt = nc.dram_tensor
        self.x_d = dt("x", [NSEQ, S, D], F32, kind="ExternalInput").ap()
        self.pos_d = dt("posr", [NSEQ, 32, S], I32, kind="ExternalInput").ap()
        self.wst_d = dt("wst", [128, self.WTOT], F32, kind="ExternalInput").ap()
        self.prm_d = dt("prm", [128, self.NPRM], F32, kind="ExternalInput").ap()
        self.rowp_d = dt("rowp", [128, 1024], F32, kind="ExternalInput").ap()
        self.wsT_d = dt("wsT", [128, 1024], F32, kind="ExternalInput").ap()
        self.wbf_d = dt("wbf", [128, self.WTOT], BF16, kind="Internal").ap()
        self.out_d = dt("out", [NSEQ, S, D], F32, kind="ExternalOutput").ap()
        if dbg:
            self.dbg_h = dt("dbg_h", [128, 8 * TB], F32, kind="ExternalOutput").ap()
            self.dbg_b = dt("dbg_b", [128, 24 * TB], BF16, kind="ExternalOutput").ap()
            self.dbg_k = dt("dbg_k", [128, 8 * S], BF16, kind="ExternalOutput").ap()
            self.dbg_v = dt("dbg_v", [128, (S // 128) * 512], BF16, kind="ExternalOutput").ap()
            self.dbg_n = dt("dbg_n", [128, 8 * TB], BF16, kind="ExternalOutput").ap()
        sb = nc.alloc_sbuf_tensor
        self.hres = sb("hres", [128, 8 * TB], F32)
        self.kc = sb("kc", [128, 8 * S], BF16)
        self.vc = sb("vc", [128, (S // 128) * 512], BF16)
        self.ident = sb("ident", [128, 128], F32)
        self.ones_bf = sb("ones_bf", [128, 128], BF16)
        self.ones_f = sb("ones_f", [128, 128], F32)
        self.tri = sb("tri", [128, 128], BF16)
        self.wsTb = sb("wsTb", [128, 1024], BF16)
        self.rowp = sb("rowp_sb", [128, 1024], F32)
        self.prm = sb("prm_sb", [128, self.NPRM], F32)
        self.cneg = sb("cneg", [128, 4], F32)
        self.lst = sb("lst", [128, 4], F32)
        self.xlh = sb("xlh", [128, 12], F32)
        self.ghalo = sb("ghalo", [128, 2 * NFC * 2], F32)
        self.small = sb("small", [128, 96], F32)
        self.lruc = sb("lruc", [128, 12], F32)
        self.lruC = sb("lruC", [128, 4 * TB], F32)
        self.hn = sb("hn", [128, 8 * TB], BF16)
        self.bigb = sb("bigb", [128, 24 * TB], BF16)
        self.nrm = sb("nrm", [128, 3 * TB], BF16)
        self.fbuf = sb("fbuf", [128, 4096], F32)
        self.post = sb("post", [128, TB], I32)
        self.tmp = [sb("tmp%d" % i, [128, TW], F32) for i in range(NTMP)]
        self.pT = [sb("pT%d" % i, [128, TB], BF16) for i in range(4)]
        self.ring = sb("ring", [128, RING_COLS], BF16)
        self.cs = sb("cs", [128, 2 * TB], F32)
        self.gg = sb("gg", [128, 4 * TB], F32)
        self.xcp = sb("xcp", [128, 4 * TB], F32)
        self.rsb = sb("rsb", [128, TB], F32)
        self.banks = [nc.alloc_psum_tensor("bank%d" % i, [128, 512], F32) for i in range(8)]
        self._tmp_i = 0
        self._bank_i = 0
        self._pT_i = 0
        self.ring_pos = 0
        self.resident = {}

    def T(self):
        t = self.tmp[self._tmp_i % NTMP]
        self._tmp_i += 1
        return t

    def bank(self, pool=None):
        pool = pool or range(8)
        pool = list(pool)
        b = self.banks[pool[self._bank_i % len(pool)]]
        self._bank_i += 1
        return b

    def prmc(self, name, i=0, rows=slice(0, 128)):
        o = self.poff[name] + i
        return self.prm[rows, o:o + 1]

    def TT(self, eng, out, a, b, op):
        self.P._record(eng, lambda e, out=out, a=a, b=b, op=op: e.tensor_tensor(out=out, in0=a, in1=b, op=op), [a, b], [out])

    def TS(self, eng, out, a, s1, s2, op0, op1=None):
        rd = [a] + [s for s in (s1, s2) if isap(s)]
        if op1 is None:
            self.P._record(eng, lambda e, out=out, a=a, s1=s1, op0=op0: e.tensor_scalar(out=out, in0=a, scalar1=s1, scalar2=None, op0=op0), rd, [out])
        else:
            self.P._record(eng, lambda e, out=out, a=a, s1=s1, s2=s2, op0=op0, op1=op1: e.tensor_scalar(out=out, in0=a, scalar1=s1, scalar2=s2, op0=op0, op1=op1), rd, [out])

    def STT(self, out, a, s, b, op0, op1):
        rd = [a, b] + ([s] if isap(s) else [])
        self.P._record("dve", lambda e, out=out, a=a, s=s, b=b, op0=op0, op1=op1: e.scalar_tensor_tensor(out=out, in0=a, scalar=s, in1=b, op0=op0, op1=op1), rd, [out])

    def ACT(self, out, a, func, bias=None, scale=None):
        rd = [a] + [s for s in (bias, scale) if isap(s)]
        kw = {}
        if bias is not None:
            kw["bias"] = bias
        if scale is not None:
            kw["scale"] = scale
        self.P._record("act", lambda e, out=out, a=a, func=func, kw=kw: e.activation(out=out, in_=a, func=func, **kw), rd, [out])

    def CP(self, eng, out, a):
        if eng == "act":
            self.P._record("act", lambda e, out=out, a=a: e.copy(out=out, in_=a), [a], [out])
        else:
            self.P._record(eng, lambda e, out=out, a=a: e.tensor_copy(out=out, in_=a), [a], [out])

    def MM(self, out, lhsT, rhs, start=True, stop=True):
        self.P._record("pe", lambda e, out=out, lhsT=lhsT, rhs=rhs, start=start, stop=stop: e.matmul(out, lhsT=lhsT, rhs=rhs, start=start, stop=stop), [lhsT, rhs], [out], tag=getattr(self, "phase", ""))

    def TR(self, out, a):
        self.P._record("pe", lambda e, out=out, a=a: e.transpose(out, a, self.ident[:]), [a, self.ident[:]], [out], tag=getattr(self, "phase", ""))

    def MEMSET(self, eng, ap, v):
        self.P._record(eng, lambda e, ap=ap, v=v: e.memset(ap, v), [], [ap])

    def DMA(self, q, out, in_):
        self.P._record(q, lambda e, out=out, in_=in_: e.dma_start(out=out, in_=in_), [in_], [out], is_dma=True)

    def W(self, name):
        o, c, pi = self.woff[name]
        pstart, pcols, _ = self.pieces[pi]
        if pi not in self.resident:
            if self.ring_pos + pcols > RING_COLS:
                self.ring_pos = 0
            base = self.ring_pos
            self.ring_pos += pcols
            for k in list(self.resident):
                kb = self.resident[k]
                kc_ = self.pieces[k][1]
                if kb < base + pcols and base < kb + kc_:
                    del self.resident[k]
            self.resident[pi] = base
            self.DMA("sp", self.ring[:, base:base + pcols], self.wbf_d[:, pstart:pstart + pcols])
        return self.resident[pi] + (o - pstart)

    def Wap(self, name, kc, M, rows=slice(0, 128), m0=0, m1=None):
        base = self.W(name)
        m1 = M if m1 is None else m1
        return self.ring[rows, base + kc * M + m0: base + kc * M + m1]

    def rmsnorm(self, src, nch, gname, dst, inv_n, dst_is_src=False):
        sq = self._sq
        self.rmsnorm_a(src, nch, sq)
        self.rmsnorm_b(src, nch, gname, dst, inv_n, sq)

    def rmsnorm_a(self, src, nch, sq):
        for c in range(nch):
            self.ACT(sq(c), src(c), AF.Square)

    def rmsnorm_b(self, src, nch, gname, dst, inv_n, sq):
        self._sq = sq
        ps = self.bank()
        for c in range(nch):
            self.MM(ps[:, :], self.ones_bf[:, :], self.sqb(dst, c), start=(c == 0), stop=(c == nch - 1))
        sd = self.T()
        self.ACT(sd[:, 0:TB], ps[:, :], AF.Sqrt, bias=self.eps_ap, scale=inv_n)
        rs = self.T()
        self.P._record("dve", lambda e, o=rs[:, 0:TB], a=sd[:, 0:TB]: e.reciprocal(out=o, in_=a), [sd[:, 0:TB]], [rs[:, 0:TB]])
        for c in range(nch):
            self.STT(dst(c), src(c), self.prmc(gname, c), rs[:, 0:TB], ALU.mult, ALU.mult)

    def sqb(self, dst, c):
        return self._sq(c)

    def build(self):
        P, nc = self.P, self.nc
        S, NSEQ = self.S, self.NSEQ
        hres, hn, bigb = self.hres, self.hn, self.bigb
        H = lambda c: hres[:, c * TB:(c + 1) * TB]
        HN = lambda c: hn[:, c * TB:(c + 1) * TB]
        BB = lambda c, rows=slice(0, 128): bigb[rows, c * TB:(c + 1) * TB]
        self.eps_t = nc.alloc_sbuf_tensor("eps_t", [128, 1], F32)
        self.eps_ap = self.eps_t[:, 0:1]
        self.MEMSET("pool", self.eps_t[:, :], EPS)
        self.qtr_t = nc.alloc_sbuf_tensor("qtr_t", [128, 1], F32)
        self.qtr_ap = self.qtr_t[:, 0:1]
        self.MEMSET("pool", self.qtr_t[:, :], 0.25)
        self.mhalf_t = nc.alloc_sbuf_tensor("mhalf_t", [128, 1], F32)
        self.mhalf_ap = self.mhalf_t[:, 0:1]
        self.MEMSET("pool", self.mhalf_t[:, :], -0.5)
        self.MEMSET("pool", self.ones_bf[:, :], 1.0)
        self.MEMSET("pool", self.ones_f[:, :], 1.0)
        P._record("pool", lambda e: e.affine_select(out=self.ident[:, :], in_=self.ones_f[:, :], pattern=[[-1, 128]], compare_op=ALU.is_equal, fill=0.0, base=0, channel_multiplier=1), [self.ones_f[:, :]], [self.ident[:, :]])
        P._record("pool", lambda e: e.affine_select(out=self.tri[:, :], in_=self.ones_f[:, :], pattern=[[1, 128]], compare_op=ALU.is_ge, fill=0.0, base=0, channel_multiplier=-1), [self.ones_f[:, :]], [self.tri[:, :]])
        self.DMA("sp", self.fbuf[:, 0:1024], self.wsT_d)
        self.DMA("sp", self.prm[:, :], self.prm_d)
        self.DMA("sp", self.rowp[:, :], self.rowp_d)
        for g in range(8):
            P._record("pool", lambda e, g=g: e.affine_select(out=self.wsTb[:, g * 128:(g + 1) * 128], in_=self.fbuf[:, g * 128:(g + 1) * 128], pattern=[[1, 128]], compare_op=ALU.is_ge, fill=0.0, base=0, channel_multiplier=-1), [self.fbuf[:, g * 128:(g + 1) * 128]], [self.wsTb[:, g * 128:(g + 1) * 128]])
        for half in range(2):
            ps = self.bank()
            for j in range(4):
                g = half * 4 + j
                self.MM(ps[:, j * 128:(j + 1) * 128], self.ones_bf[:, :], self.wsTb[:, g * 128:(g + 1) * 128])
            for j in range(4):
                g = half * 4 + j
                self.STT(self.rowp[:, g * 128:(g + 1) * 128], ps[:, j * 128:(j + 1) * 128], self.prmc("ln_b", g), self.rowp[:, g * 128:(g + 1) * 128], ALU.mult, ALU.add)
        lam = self.prm[:, self.poff["lam"]:self.poff["lam"] + 4]
        self.ACT(self.small[:, 0:4], lam, AF.Exp, scale=-1.0)
        self.ACT(self.small[:, 4:8], self.small[:, 0:4], AF.Ln, bias=self.ones_f[:, 0:1])
        self.TS("dve", self.cneg[:, :], self.small[:, 4:8], -8.0, None, ALU.mult)
        self.TS("dve", self.lruc[:, 0:4], self.prm[:, self.poff["b_a"]:self.poff["b_a"] + 4], 0.5, None, ALU.mult)
        self.TS("dve", self.lruc[:, 4:8], self.prm[:, self.poff["b_x"]:self.poff["b_x"] + 4], 0.5, None, ALU.mult)
        self.TS("dve", self.lruc[:, 8:12], self.small[:, 4:8], -4.0, None, ALU.mult)
        P._record("dve", lambda e: e.memzero(self.bigb[:, 0:(24 if self.dbg else 8) * TB]), [], [self.bigb[:, 0:(24 if self.dbg else 8) * TB]])
        P._record("dve", lambda e: e.memzero(self.kc[:, :]), [], [self.kc[:, :]])
        if self.dbg:
            self.MEMSET("pool", self.vc[:, :], 0.0)
            self.MEMSET("pool", self.hn[:, :], 0.0)
        self.MEMSET("pool", self.lst[:, :], 0.0)
        self.MEMSET("pool", self.xlh[:, :], 0.0)
        self.MEMSET("pool", self.ghalo[:, :], 0.0)
        CH = 2048
        for a in range(0, self.WTOT, CH):
            b = min(a + CH, self.WTOT)
            self.DMA("pool", self.wbf_d[:, a:b], self.wst_d[:, a:b])
        for s in range(NSEQ):
            if s > 0:
                self.MEMSET("pool", self.lst[:, :], 0.0)
                self.MEMSET("pool", self.xlh[:, :], 0.0)
                self.MEMSET("pool", self.ghalo[:, :], 0.0)
            for b in range(self.NBLK):
                if s == 0 and b == 0:
                    self.load_x(0, 0)
                self.block(s, b)
        P.emit()
        return nc

    def XS(self, tt):
        t = self.gg if tt < 2 else self.xcp
        return t[:, (tt % 2) * 1024:(tt % 2 + 1) * 1024]

    def load_x(self, s, b):
        t0 = b * TB
        for tt in range(4):
            self.DMA("sp", self.XS(tt), self.x_d[s, t0 + tt * 128:t0 + (tt + 1) * 128, :])
        R96 = slice(64, 96)
        self.DMA("sp", self.post[R96, :], self.pos_d[s, :, t0:t0 + TB])
        yv = self.T(); k1 = self.T(); k2 = self.T()
        sin_t = self.cs[:, 0:TB]; cos_t = self.cs[:, TB:2 * TB]
        V = lambda t: t[R96, 0:TB]
        self.TS("dve", V(yv), self.post[R96, :], self.prmc("invf", 0, R96), None, ALU.mult)
        self.TS("dve", V(k1), V(yv), MAGIC, None, ALU.add)
        self.TS("dve", V(k1), V(k1), MAGIC, None, ALU.subtract)
        self.TT("dve", V(k1), V(yv), V(k1), ALU.subtract)
        self.ACT(V(sin_t), V(k1), AF.Sin, scale=self.prmc("sgn2pi", 0, R96))
        self.TS("dve", V(yv), V(yv), 0.25, None, ALU.add)
        self.TS("dve", V(k2), V(yv), MAGIC, None, ALU.add)
        self.TS("dve", V(k2), V(k2), MAGIC, None, ALU.subtract)
        self.TT("dve", V(k2), V(yv), V(k2), ALU.subtract)
        self.ACT(V(cos_t), V(k2), AF.Sin, scale=6.2831845)

    def block(self, s, b):
        P = self.P
        self.PL = "dve" if (s == 0 and b == 0) else "pool"
        S = self.S
        t0 = b * TB
        hres, hn, bigb, fbuf = self.hres, self.hn, self.bigb, self.fbuf
        H = lambda c: hres[:, c * TB:(c + 1) * TB]
        HN = lambda c: hn[:, c * TB:(c + 1) * TB]
        BB = lambda c, rows=slice(0, 128): bigb[rows, c * TB:(c + 1) * TB]
        R96 = slice(64, 96)
        hres3 = hres[:, :].rearrange("p (c t) -> p c t", t=TB)
        self.phase = 'xin'
        for tt in range(4):
            stg = self.XS(tt)
            for half in range(2):
                ps = self.bank()
                for j in range(4):
                    c = half * 4 + j
                    self.TR(ps[:, j * 128:(j + 1) * 128], stg[:, c * 128:(c + 1) * 128])
                self.CP("act" if half == 0 else "dve", hres3[:, half * 4:half * 4 + 4, tt * 128:(tt + 1) * 128],
                        ps[:, :].rearrange("p (c t) -> p c t", t=128))
        sin_t = self.cs[:, 0:TB]; cos_t = self.cs[:, TB:2 * TB]
        V = lambda t: t[R96, 0:TB]
        self.phase = 'zproj'
        rs = self.rsb[:, 0:TB]
        for c in range(8):
            if c % 2 == 0:
                self.TS("dve", HN(c), H(c), self.prmc("ab_norm", c), None, ALU.mult)
            else:
                self.ACT(HN(c), H(c), AF.Copy, scale=self.prmc("ab_norm", c))
        for c in range(8):
            if c % 2 == 0:
                self.ACT(BB(8 + c), H(c), AF.Square)
            else:
                self.TT(self.PL, BB(8 + c), H(c), H(c), ALU.mult)

        def zproj(name, M, ps):
            for kc in range(8):
                self.MM(ps[0:M, :], self.Wap(name, kc, M), HN(kc), start=(kc == 0), stop=(kc == 7))

        CQ = lambda c: fbuf[:, 2048 + c * TB:2048 + (c + 1) * TB]
        CKV = fbuf[:, 2048 + 2 * TB:2048 + 3 * TB]
        for c in range(2):
            ps = self.bank()
            zproj("in_cq%d" % c, 128, ps)
            if c == 0:
                pss = self.bank()
                for c_ in range(8):
                    self.MM(pss[:, :], self.ones_bf[:, :], BB(8 + c_), start=(c_ == 0), stop=(c_ == 7))
                sd = self.T()
                self.ACT(sd[:, 0:TB], pss[:, :], AF.Sqrt, bias=self.eps_ap, scale=1.0 / 1024)
                P._record("dve", lambda e, o=rs, a=sd[:, 0:TB]: e.reciprocal(out=o, in_=a), [sd[:, 0:TB]], [rs])
            self.TT("dve", CQ(c), ps[:, :], rs, ALU.mult)
        ps = self.bank()
        zproj("in_ckv", 128, ps)
        self.TT("dve", CKV, ps[:, :], rs, ALU.mult)
        ps_kpe = self.bank()
        zproj("in_kpe", 128, ps_kpe)
        ps_kpr = self.bank()
        zproj("in_kpr", 128, ps_kpr)
        t1 = self.T(); t2 = self.T()
        self.TT("dve", V(t1), ps_kpe[R96, :], V(cos_t), ALU.mult)
        self.TT("dve", V(t2), ps_kpr[R96, :], V(sin_t), ALU.mult)
        kpb = self.pT[3]
        self.TT("dve", V(t1), V(t1), V(t2), ALU.add)
        self.TT("dve", kpb[R96, :], V(t1), rs[R96, :], ALU.mult)
        for h in range(8):
            self.CP("act" if h % 2 == 0 else self.PL, self.kc[R96, h * S + t0:h * S + t0 + TB], kpb[R96, :])
        self.phase = 'qkvnorm'
        NR = lambda c: self.nrm[:, c * TB:(c + 1) * TB]
        sq_q = lambda c: NR(c)
        sq_kv = lambda c: NR(2)
        self.rmsnorm_a(CQ, 2, sq_q)
        self.rmsnorm_a(lambda c: CKV, 1, sq_kv)
        self.phase = 'lru'
        xlts = []
        for c in range(4):
            ps_x = self.bank()
            zproj("in_xl%d" % c, 128, ps_x)
            ps_g = self.bank()
            zproj("in_gt%d" % c, 128, ps_g)
            if c == 0:
                self.phase = 'qkvnorm'
                self.rmsnorm_b(CQ, 2, "q_norm", NR, 1.0 / 256, sq_q)
                self.rmsnorm_b(lambda c_: CKV, 1, "kv_norm", lambda c_: NR(2), 1.0 / 128, sq_kv)
                self.phase = 'lru'
            xlt = self.T()
            xlts.append(xlt)
            self.CP(self.PL, xlt[:, 0:3], self.xlh[:, c * 3:c * 3 + 3])
            self.TT("dve", xlt[:, 3:3 + TB], ps_x[:, :], rs, ALU.mult)
            self.CP(self.PL, self.xlh[:, c * 3:c * 3 + 3], xlt[:, TB:TB + 3])
            ggc = self.gg[:, c * TB:(c + 1) * TB]
            self.TT("dve", ggc, ps_g[:, :], rs, ALU.mult)
            self.ACT(ggc, ggc, GELU)
            xc = self.xcp[:, c * TB:(c + 1) * TB]
            self.ACT(xc, xlt[:, 0:TB], AF.Identity, bias=self.prmc("lconv_b", c), scale=self.prmc("lconv_w", c * 4))

        def lru_steps(c):
            xc = self.xcp[:, c * TB:(c + 1) * TB]
            gg = self.gg[:, c * TB:(c + 1) * TB]
            xcb = BB(20 + c)
            gbk = self.banks[3]
            A = fbuf[:, c * TB:(c + 1) * TB]
            Bt = fbuf[:, (4 + c) * TB:(5 + c) * TB]
            C = self.lruC[:, c * TB:(c + 1) * TB]
            Hs = Bt

            def s1_():
                ba = self.W("wa%d" % c)
                self.MM(gbk[:, :], self.ring[:, ba:ba + 128], xcb)
                self.ACT(A, gbk[:, :], AF.Tanh, bias=self.lruc[:, c:c + 1], scale=0.5)

            def s2_():
                bx_ = self.W("wx%d" % c)
                self.MM(gbk[:, :], self.ring[:, bx_:bx_ + 128], xcb)
                self.ACT(C, gbk[:, :], AF.Tanh, bias=self.lruc[:, 4 + c:5 + c], scale=0.5)

            def s3_():
                self.ACT(A, A, AF.Exp, bias=self.lruc[:, 8 + c:9 + c], scale=self.lruc[:, 8 + c:9 + c])

            def s4_():
                self.ACT(Bt, A, AF.Square)

            def s5_():
                self.ACT(Bt, Bt, AF.Sqrt, bias=self.qtr_ap, scale=-0.25)

            def s6_():
                self.STT(C, C, 1.0, xc, ALU.add, ALU.mult)
                self.TT("dve", C, C, Bt, ALU.mult)

            def s7_():
                P._record("dve", lambda e, o_=Hs, a=A, bb=C, ini=self.lst[:, c:c + 1]: e.tensor_tensor_scan(out=o_, data0=a, data1=bb, initial=ini, op0=ALU.mult, op1=ALU.add),
                          [A, C, self.lst[:, c:c + 1]], [Hs])
                self.CP(self.PL, self.lst[:, c:c + 1], Hs[:, TB - 1:TB])

            def s8_():
                self.TT("dve", BB(16 + c), Hs, gg, ALU.mult)

            return [s1_, s2_, s3_, s4_, s5_, s6_, s7_, s8_]

        self.phase = 'qheads'
        self.ACT(self.small[:, 91:92], self.eps_ap, AF.Exp)
        for h in range(8):
            psq = self.bank()
            psr = self.bank()
            for kc in range(2):
                self.MM(psq[:, :], self.Wap("wq%d" % h, kc, 128), NR(kc), start=(kc == 0), stop=(kc == 1))
            for kc in range(2):
                self.MM(psr[:, :], self.Wap("wqr%d" % h, kc, 128), NR(kc), start=(kc == 0), stop=(kc == 1))
            self.CP("act", BB(h, slice(0, 64)), psq[0:64, :])
            a1 = self.lruC[:, (h % 2) * 2 * TB:(h % 2) * 2 * TB + TB]
            a2 = self.lruC[:, (h % 2) * 2 * TB + TB:(h % 2) * 2 * TB + 2 * TB]
            self.TT("dve", a1[R96, :], psq[R96, :], V(cos_t), ALU.mult)
            self.TT("dve", a2[R96, :], psr[R96, :], V(sin_t), ALU.mult)
            self.TT(self.PL, BB(h, R96), a1[R96, :], a2[R96, :], ALU.add)
        self.phase = 'kv'
        for h in range(8):
            ps = self.bank()
            base = self.W("wk%d" % h)
            self.MM(ps[:, :], self.ring[:, base:base + 128], NR(2))
            self.CP("act", self.kc[0:64, h * S + t0:h * S + t0 + TB], ps[0:64, :])
        basev = self.W("wv")
        for tt in range(4):
            ps = self.bank()
            self.MM(ps[:, :], NR(2)[:, tt * 128:(tt + 1) * 128], self.ring[:, basev:basev + 512])
            vt = (t0 // 128 + tt) * 512
            self.CP("dve", self.vc[:, vt:vt + 512], ps[:, :])
        def conv_step(c):
            def f():
                xlt = xlts[c]
                xc = self.xcp[:, c * TB:(c + 1) * TB]
                for k in range(1, 4):
                    self.STT(xc, xlt[:, k:k + TB], self.prmc("lconv_w", c * 4 + k), xc, ALU.mult, ALU.add)
                self.CP("act", BB(20 + c), xc)
            return f
        self.phase = 'attn'
        nkt = 4 * b + 4
        items = [(h, kt) for h in range(8) for kt in range(nkt)]
        LA = 2
        pend = {}

        def issue_s(i):
            h, kt = items[i]
            j = kt - 4 * b
            c0 = max(j, 0) * 128
            n = TB - c0
            s_ps = self.banks[i % 3]
            self.MM(s_ps[:, 0:n], self.kc[:, h * S + kt * 128:h * S + (kt + 1) * 128], BB(h)[:, c0:TB])
            pT = self.pT[i % 4]
            self.ACT(pT[:, 0:n], s_ps[:, 0:n], AF.Exp, scale=SCALE)
            if j >= 0:
                self.TT(self.PL, pT[:, 0:128], pT[:, 0:128], self.tri[:, :], ALU.mult)
            pend[i] = (pT, c0, n)

        def issue_pv(i):
            h, kt = items[i]
            pT, c0, n = pend.pop(i)
            hp = h // 2
            o_ps = self.banks[4 + (h % 2)]
            d_ps = self.banks[6 + (h % 2)]
            self.MM(o_ps[:, c0:TB], self.vc[:, kt * 512 + hp * 128:kt * 512 + (hp + 1) * 128], pT[:, 0:n], start=(kt == 0), stop=(kt == nkt - 1))
            self.MM(d_ps[:, c0:TB], self.ones_bf[:, :], pT[:, 0:n], start=(kt == 0), stop=(kt == nkt - 1))
            if kt == nkt - 1:
                rs = slice((h % 2) * 64, (h % 2) * 64 + 64)
                rd = self.T()
                P._record("dve", lambda e, o=rd[rs, 0:TB], a=d_ps[rs, :]: e.reciprocal(out=o, in_=a), [d_ps[rs, :]], [rd[rs, 0:TB]])
                self.TT("dve", BB(8 + hp, rs), o_ps[rs, :], rd[rs, 0:TB], ALU.mult)

        NI = len(items)
        sched = {}
        for c in range(4):
            sched.setdefault(c, []).append(conv_step(c))
        steps = [lru_steps(c) for c in range(4)]
        last = NI + LA - 1
        q4 = NI // 4
        for c in range(4):
            for k in range(4):
                sched.setdefault(min(last, c * q4 + 5 + k), []).append(steps[c][k])
        burst = min(last, 3 * q4 + 9)
        for c in range(4):
            sched.setdefault(burst, []).append(steps[c][4])
        rest = max(1, (last - burst) // 13)
        pos = burst
        for c in range(4):
            for k in range(5, 8):
                pos = min(last, pos + rest)
                sched.setdefault(pos, []).append(steps[c][k])
        for i in range(NI + LA):
            if i < NI:
                issue_s(i)
            if i - LA >= 0:
                issue_pv(i - LA)
            for fn in sched.get(i, []):
                ph = self.phase
                self.phase = 'lru2'
                fn()
                self.phase = ph
        self.phase = 'oproj'
        for grp in range(2):
            pss = [self.banks[grp * 4 + j] for j in range(4)]
            srcs = [("wom%d" % c, BB(8 + c)) for c in range(4)] + [("wol%d" % c, BB(16 + c)) for c in range(4)]
            for ci, (wn, rhs) in enumerate(srcs):
                base = self.W(wn)
                for j in range(4):
                    oc = grp * 4 + j
                    self.MM(pss[j][:, :], self.ring[:, base + oc * 128:base + (oc + 1) * 128], rhs, start=(ci == 0), stop=(ci == 7))
            for j in range(4):
                oc = grp * 4 + j
                self.TT("dve", H(oc), H(oc), pss[j][:, :], ALU.add)
                self.TS("dve", HN(oc), H(oc), self.prmc("ffn_norm0", oc), None, ALU.mult)
        if self.dbg == 1:
            return self.dump()
        self.phase = 'ffn0'
        self.ffn(0)
        if self.dbg == 2:
            return self.dump()
        self.phase = 'sgu'
        self.rmsnorm_b(H, 8, "c_norm", HN, 1.0 / 1024, lambda c: HN(c))
        bias2 = self.rowp
        bigb3 = bigb[:, :].rearrange("p (c t) -> p c t", t=TB)
        VSB = lambda tt: fbuf[:, tt * 1024:(tt + 1) * 1024]
        VLN = lambda tt: bigb[:, (16 + 2 * tt) * TB:(16 + 2 * tt) * TB + 1024]
        def sgu_u():
            for uc in range(8):
                ps = self.bank()
                for kc in range(8):
                    self.MM(ps[:, :], self.Wap("cu%d" % uc, kc, 128), HN(kc), start=(kc == 0), stop=(kc == 7))
                self.ACT(BB(uc), ps[:, :], GELU)

        def sgu_v(tt):
            vsb = VSB(tt)
            for hf in range(2):
                ps = self.bank()
                for kc in range(8):
                    base = self.W("cv%d" % hf)
                    self.MM(ps[:, :], HN(kc)[:, tt * 128:(tt + 1) * 128], self.ring[:, base + kc * 512:base + (kc + 1) * 512], start=(kc == 0), stop=(kc == 7))
                self.ACT(vsb[:, hf * 512:(hf + 1) * 512], ps[:, :], GELU)
            st = self.small
            o = 8 + tt * 20
            for hf in range(2):
                P._record("dve", lambda e, o_=st[:, o + hf * 6:o + 6 + hf * 6], a=vsb[:, hf * 512:(hf + 1) * 512]: e.bn_stats(out=o_, in_=a), [vsb[:, hf * 512:(hf + 1) * 512]], [st[:, o + hf * 6:o + 6 + hf * 6]])
            P._record("dve", lambda e, o_=st[:, o + 12:o + 14], a=st[:, o:o + 12]: e.bn_aggr(out=o_, in_=a), [st[:, o:o + 12]], [st[:, o + 12:o + 14]])
            self.TS("dve", st[:, o + 14:o + 15], st[:, o + 13:o + 14], EPS, None, ALU.add)
            self.TT("pool", st[:, o + 15:o + 16], st[:, o + 14:o + 15], self.mhalf_ap, ALU.pow)
            self.TS("dve", VLN(tt), vsb, st[:, o + 12:o + 13], st[:, o + 15:o + 16], ALU.subtract, ALU.mult)

        def sgu_sp(tt):
            vln = VLN(tt)
            for half in range(2):
                ps = self.bank()
                for j in range(4):
                    g = half * 4 + j
                    self.MM(ps[:, j * 128:(j + 1) * 128], vln[:, g * 128:(g + 1) * 128], self.wsTb[:, g * 128:(g + 1) * 128])
                sm = self.T()
                for j in range(4):
                    g = half * 4 + j
                    self.STT(sm[:, j * 128:(j + 1) * 128], ps[:, j * 128:(j + 1) * 128], self.prmc("ln_g", g), bias2[:, g * 128:(g + 1) * 128], ALU.mult, ALU.add)
                self.TT("dve", bigb3[:, 8 + half * 4:8 + half * 4 + 4, tt * 128:(tt + 1) * 128],
                        sm[:, 0:TB].rearrange("p (c t) -> p c t", t=128),
                        bigb3[:, half * 4:half * 4 + 4, tt * 128:(tt + 1) * 128], ALU.mult)

        sgu_u()
        sgu_v(0)
        sgu_v(1)
        sgu_sp(0)
        sgu_v(2)
        sgu_sp(1)
        sgu_v(3)
        sgu_sp(2)
        sgu_sp(3)
        for oc in range(8):
            ps = self.bank()
            for g in range(8):
                self.MM(ps[:, :], self.Wap("co%d" % oc, g, 128), BB(8 + g), start=(g == 0), stop=(g == 7))
            self.TT("dve", H(oc), H(oc), ps[:, :], ALU.add)
            self.TS("dve", HN(oc), H(oc), self.prmc("ffn_norm1", oc), None, ALU.mult)
        if self.dbg == 3:
            return self.dump()
        self.phase = 'ffn1'
        nb_, ns_ = b + 1, s
        if nb_ == self.NBLK:
            nb_, ns_ = 0, s + 1
        if ns_ < self.NSEQ:
            self.load_x(ns_, nb_)
        self.ffn(1)
        if self.dbg == 4:
            return self.dump()
        self.phase = 'fin'
        self.rmsnorm_b(H, 8, "final_norm", H, 1.0 / 1024, lambda c: HN(c))
        for tt in range(4):
            stg = fbuf[:, (tt % 2) * 1024:(tt % 2 + 1) * 1024]
            for half in range(2):
                ps = self.bank()
                for j in range(4):
                    c = half * 4 + j
                    self.TR(ps[:, j * 128:(j + 1) * 128], H(c)[:, tt * 128:(tt + 1) * 128])
                self.CP("act" if half == 0 else "dve", stg[:, half * 512:(half + 1) * 512], ps[:, :])
            self.DMA("act", self.out_d[s, t0 + tt * 128:t0 + (tt + 1) * 128, :], stg)

    def dump(self):
        self.DMA("sp", self.dbg_h, self.hres[:, :])
        self.DMA("sp", self.dbg_b, self.bigb[:, :])
        self.DMA("sp", self.dbg_k[0:96, :], self.kc[0:96, :])
        self.DMA("sp", self.dbg_v, self.vc[:, :])
        self.DMA("sp", self.dbg_n, self.hn[:, :])

    def ffn(self, l):
        P = self.P
        hres, hn, bigb, fbuf = self.hres, self.hn, self.bigb, self.fbuf
        H = lambda c: hres[:, c * TB:(c + 1) * TB]
        HN = lambda c: hn[:, c * TB:(c + 1) * TB]
        BB = lambda c: bigb[:, c * TB:(c + 1) * TB]
        rs = self.rsb[:, 0:TB]
        for c in range(8):
            self.ACT(BB(14 + c), H(c), AF.Square)
        st = {}

        def stage_mm(fc):
            ps_g = self.bank()
            for kc in range(8):
                self.MM(ps_g[:, :], self.Wap("fg%d_%d" % (l, fc), kc, 128), HN(kc), start=(kc == 0), stop=(kc == 7))
            if fc == 0:
                pss = self.bank()
                for c in range(8):
                    self.MM(pss[:, :], self.ones_bf[:, :], BB(14 + c), start=(c == 0), stop=(c == 7))
                sd = self.T()
                self.ACT(sd[:, 0:TB], pss[:, :], AF.Sqrt, bias=self.eps_ap, scale=1.0 / 1024)
                P._record("dve", lambda e, o=rs, a=sd[:, 0:TB]: e.reciprocal(out=o, in_=a), [sd[:, 0:TB]], [rs])
            ps_u = self.bank()
            for kc in range(8):
                self.MM(ps_u[:, :], self.Wap("fu%d_%d" % (l, fc), kc, 128), HN(kc), start=(kc == 0), stop=(kc == 7))
            st[fc] = {"g": ps_g, "u": ps_u}

        def stage_ab(fc):
            d_ = st[fc]
            gb = fbuf[:, 2048 + (fc % 3) * 520:2048 + (fc % 3) * 520 + 514]
            gh = self.ghalo[:, (l * NFC + fc) * 2:(l * NFC + fc) * 2 + 2]
            self.CP(self.PL, gb[:, 0:2], gh)
            self.TT("dve", gb[:, 2:2 + TB], d_["g"][:, :], rs, ALU.mult)
            self.CP(self.PL, gh, gb[:, TB:TB + 2])
            t = self.T()
            self.ACT(t[:, 0:TB], gb[:, 0:TB], AF.Identity, bias=self.prmc("fconv_b%d" % l, fc), scale=self.prmc("fconv_w%d" % l, fc * 3))
            d_["gb"] = gb
            d_["t"] = t

        def stage_cd(fc):
            d_ = st[fc]
            gb, t = d_["gb"], d_["t"]
            self.STT(t[:, 0:TB], gb[:, 1:1 + TB], self.prmc("fconv_w%d" % l, fc * 3 + 1), t[:, 0:TB], ALU.mult, ALU.add)
            self.STT(t[:, 0:TB], gb[:, 2:2 + TB], self.prmc("fconv_w%d" % l, fc * 3 + 2), t[:, 0:TB], ALU.mult, ALU.add)
            ge = self.T()
            self.ACT(ge[:, 0:TB], t[:, 0:TB], GELU)
            d_["ge"] = ge

        def stage_e(fc):
            d_ = st.pop(fc)
            self.TT("dve", BB(fc), d_["ge"][:, 0:TB], d_["u"][:, :], ALU.mult)

        for i in range(NFC + 2):
            if i < NFC:
                stage_mm(i)
                stage_ab(i)
            if 0 <= i - 1 < NFC:
                stage_cd(i - 1)
            if 0 <= i - 2 < NFC:
                stage_e(i - 2)
        self.ACT(self.small[:, 90:91], self.eps_ap, AF.Sqrt)
        for oc in range(8):
            ps = self.bank()
            for fc in range(NFC):
                self.MM(ps[:, :], self.Wap("fd%d_%d" % (l, oc), fc, 128), BB(fc), start=(fc == 0), stop=(fc == NFC - 1))
            d = self.T()
            self.TT("dve", d[:, 0:TB], ps[:, :], rs, ALU.mult)
            self.TT(self.PL, H(oc), H(oc), d[:, 0:TB], ALU.add)
            self.ACT(HN(oc), H(oc), AF.Square)


_CACHE = {}


def get_program(NSEQ, S, dbg=0):
    key = (NSEQ, S, dbg)
    if key not in _CACHE:
        _CACHE[key] = Builder(NSEQ, S, dbg).build()
    return _CACHE[key]


def make_in_maps(inp, n_cores, NSEQ, S):
    x = np.asarray(inp["x"], np.float32)
    pos = np.asarray(inp["positions"], np.int32)
    wst = pack_weights(inp)
    prm = pack_params(inp)
    rowp = np.zeros((128, 1024), np.float32)
    rowp[:, :] = np.asarray(inp["c_b_s"][0], np.float32).reshape(1, 1024)
    ws = np.asarray(inp["c_w_s"][0], np.float32)
    wsT = np.ascontiguousarray(ws.transpose(2, 0, 1)).reshape(128, 1024)
    maps = []
    for i in range(n_cores):
        xs = np.ascontiguousarray(x[i * NSEQ:(i + 1) * NSEQ, :S])
        ps = pos[i * NSEQ:(i + 1) * NSEQ, :S]
        posr = np.ascontiguousarray(np.broadcast_to(ps[:, None, :], (NSEQ, 32, S))).astype(np.int32)
        maps.append({"x": xs, "posr": posr, "wst": wst, "prm": prm, "rowp": rowp, "wsT": wsT})
    return maps


def kernel(**inputs):
    n_cores = 8
    B, S, _ = inputs["x"].shape
    NSEQ = B // n_cores
    nc = get_program(NSEQ, S)
    maps = make_in_maps(inputs, n_cores, NSEQ, S)
    res = run_bass_kernel_spmd(nc, maps, core_ids=list(range(n_cores)))
    out = np.concatenate([np.asarray(r["out"], np.float32).reshape(NSEQ, S, D) for r in res.results], axis=0)
    return out
```
